# Optimizing a Trainium2 kernel written in Bass

```python
import jax
import jax.numpy as jnp
from jax import lax
import numpy as np

D_MODEL = 1024
BATCH = 16
SEQ = 2048
DEPTH = 2

N_META = 16
CHUNK = 32
FRONT_PAD = (CHUNK - N_META % CHUNK) % CHUNK
F_TINY = 1e-30

GLA_HEADS = 4
GLA_DK = 64
GLA_DV = 128
GLA_K = GLA_HEADS * GLA_DK
GLA_V = GLA_HEADS * GLA_DV
GLA_GATE_RANK = 16
GLA_GATE_NORM = 16.0

RW_HEADS = 8
RW_HD = 64
RW_DIM = RW_HEADS * RW_HD
RW_W_RANK = 64
RW_A_RANK = 64
RW_V_RANK = 32
RW_G_RANK = 128
RW_GN_EPS = 64e-5

HG_HEADS = 4
HG_DK = 128
HG_DV = 128
HG_K = HG_HEADS * HG_DK
HG_V = HG_HEADS * HG_DV

D_FF = 2816
CONV_W = 3
NORM_EPS = 1e-6

GLA_LAYOUT = (("gla_q", GLA_K), ("gla_k", GLA_K), ("gla_v", GLA_V), ("gla_gk", GLA_GATE_RANK), ("gla_g", GLA_V))
HG_LAYOUT = (("hg_q", HG_K), ("hg_f", HG_K), ("hg_i", HG_V), ("hg_g", HG_V))
GATE_LAYOUT = (("gate_gla", D_MODEL), ("gate_rw", D_MODEL), ("gate_hg", D_MODEL))
NON_RW_LAYOUT = GLA_LAYOUT + HG_LAYOUT + GATE_LAYOUT
RW_LAYOUT = (("rw_r", RW_DIM), ("rw_w", RW_W_RANK), ("rw_k", RW_DIM), ("rw_v", RW_DIM), ("rw_a", RW_A_RANK), ("rw_g", RW_G_RANK))
RW_VRES_LAYOUT = (("rw_vr", RW_V_RANK),)
NON_RW = sum(w for _, w in NON_RW_LAYOUT)
RW_SHIFT = sum(w for _, w in RW_LAYOUT)
W_IN = NON_RW + RW_SHIFT

kernel_name = "hybrid_gla_rwkv7_hgrn2_block"


def _rmsnorm(x, w):
    x32 = x.astype(jnp.float32)
    y = x32 * lax.rsqrt(jnp.mean(x32 * x32, axis=-1, keepdims=True) + NORM_EPS)
    return (y * w.astype(jnp.float32)).astype(x.dtype)


def _head_rmsnorm(o, w):
    return o * lax.rsqrt(jnp.mean(o * o, axis=-1, keepdims=True) + NORM_EPS) * w


def _split(p, layout):
    out, off = {}, 0
    for name, width in layout:
        out[name] = p[..., off:off + width]
        off += width
    return out


def _token_shift(p, mu):
    prev = jnp.pad(p[:, :-1], ((0, 0), (1, 0), (0, 0)))
    return p + (prev - p) * mu.astype(p.dtype)


def _chunk_gated_linear_attention(q, k, v, g):
    bsz, t, h, dk = q.shape
    dv = v.shape[-1]
    pad = ((0, 0), (FRONT_PAD, 0), (0, 0), (0, 0))
    q, k, v, g = [jnp.pad(a, pad) for a in (q, k, v, g)]
    n = (t + FRONT_PAD) // CHUNK

    def blk(a):
        return a.reshape(bsz, n, CHUNK, h, a.shape[-1]).transpose(0, 3, 1, 2, 4)

    q, k, v, g = blk(q), blk(k), blk(v), blk(g)
    b = jnp.cumsum(g, axis=3)
    b_ref = b[:, :, :, CHUNK // 2 - 1:CHUNK // 2]
    b_last = b[:, :, :, -1:]
    scores = jnp.einsum('bhnid,bhnjd->bhnij', q * jnp.exp(b - b_ref), k * jnp.exp(b_ref - b))
    causal = jnp.tril(jnp.ones((CHUNK, CHUNK), dtype=bool))
    scores = jnp.where(causal, scores, 0.0)
    o_intra = jnp.einsum('bhnij,bhnjv->bhniv', scores, v)
    inc = jnp.einsum('bhnjd,bhnjv->bhndv', k * jnp.exp(b_last - b), v)
    dec = jnp.exp(b_last[:, :, :, 0])

    def step(s, xs):
        d_n, u_n = xs
        return d_n[..., None] * s + u_n, s

    s0 = jnp.zeros((bsz, h, dk, dv), jnp.float32)
    _, s_start = lax.scan(step, s0, (jnp.moveaxis(dec, 2, 0), jnp.moveaxis(inc, 2, 0)))
    s_start = jnp.moveaxis(s_start, 0, 2)
    o_inter = jnp.einsum('bhnid,bhndv->bhniv', q * jnp.exp(b), s_start)
    o = (o_intra + o_inter).transpose(0, 2, 3, 1, 4).reshape(bsz, n * CHUNK, h, dv)
    return o[:, FRONT_PAD:]


def _gla_branch(f, gk_up, gk_bias, norm_w):
    f32 = jnp.float32
    bsz, t, _ = f["gla_q"].shape
    q = f["gla_q"].astype(f32).reshape(bsz, t, GLA_HEADS, GLA_DK) * (GLA_DK ** -0.5)
    k = f["gla_k"].astype(f32).reshape(bsz, t, GLA_HEADS, GLA_DK)
    v = f["gla_v"].astype(f32).reshape(bsz, t, GLA_HEADS, GLA_DV)
    gk = jax.nn.log_sigmoid((f["gla_gk"] @ gk_up + gk_bias).astype(f32)) / GLA_GATE_NORM
    gk = gk.reshape(bsz, t, GLA_HEADS, GLA_DK)
    o = _chunk_gated_linear_attention(q, k, v, gk)
    g = f["gla_g"].astype(f32).reshape(bsz, t, GLA_HEADS, GLA_DV)
    o = _head_rmsnorm(o, norm_w.astype(f32)) * jax.nn.silu(g)
    return o.reshape(bsz, t, GLA_V)


def _hgrn2_branch(f, lb, norm_w):
    f32 = jnp.float32
    bsz, t, _ = f["hg_f"].shape
    z = f["hg_f"].astype(f32).reshape(bsz, t, HG_HEADS, HG_DK)
    lb = lb.reshape(HG_HEADS, HG_DK)
    forget = lb + (1.0 - lb) * jax.nn.sigmoid(z)
    log_f = jnp.log(jnp.maximum(forget, F_TINY))
    k = (1.0 - lb) * jax.nn.sigmoid(-z)
    q = f["hg_q"].astype(f32).reshape(bsz, t, HG_HEADS, HG_DK)
    i = f["hg_i"].astype(f32).reshape(bsz, t, HG_HEADS, HG_DV)
    o = _chunk_gated_linear_attention(q, k, i, log_f)
    g = f["hg_g"].astype(f32).reshape(bsz, t, HG_HEADS, HG_DV)
    o = _head_rmsnorm(o, norm_w.astype(f32)) * jax.nn.silu(g)
    return o.reshape(bsz, t, HG_V)


def _rwkv7_scan(r, w, k, v, kk, a):
    bsz, _, h, n = r.shape

    def step(s, xs):
        r_t, w_t, k_t, v_t, kk_t, a_t = xs
        sa = jnp.einsum('bhvk,bhk->bhv', s, -kk_t)
        s = (s * w_t[:, :, None, :] + sa[..., None] * (kk_t * a_t)[:, :, None, :]
             + v_t[..., None] * k_t[:, :, None, :])
        return s, jnp.einsum('bhvk,bhk->bhv', s, r_t)

    s0 = jnp.zeros((bsz, h, n, n), jnp.float32)
    xs = tuple(jnp.moveaxis(a_, 1, 0) for a_ in (r, w, k, v, kk, a))
    _, y = lax.scan(step, s0, xs)
    return jnp.moveaxis(y, 0, 1)


def _rwkv7_branch(s, w0, w2, a0, a2, g2, k_k, k_a, r_k, ln_w, ln_b, v_first, v0, v2):
    f32 = jnp.float32
    bsz, t, _ = s["rw_r"].shape
    r = s["rw_r"].astype(f32)
    k = s["rw_k"].astype(f32)
    v = s["rw_v"].astype(f32)
    w_log = -jax.nn.softplus(-(w0 + jnp.tanh(s["rw_w"]) @ w2).astype(f32)) - 0.5
    decay = jnp.exp(-jnp.exp(w_log))
    a = jax.nn.sigmoid((a0 + s["rw_a"] @ a2).astype(f32))
    g = (jax.nn.sigmoid(s["rw_g"]) @ g2).astype(f32)
    if v_first is None:
        v_first = v
    else:
        v = v + (v_first - v) * jax.nn.sigmoid((v0 + s["rw_vr"] @ v2).astype(f32))
    hd = lambda x_: x_.reshape(bsz, t, RW_HEADS, RW_HD)
    kk = hd(k * k_k.astype(f32))
    kk = kk / jnp.maximum(jnp.sqrt(jnp.sum(kk * kk, axis=-1, keepdims=True)), 1e-12)
    k = k * (1.0 + (a - 1.0) * k_a.astype(f32))
    rh, kh, vh = hd(r), hd(k), hd(v)
    y = _rwkv7_scan(rh, hd(decay), kh, vh, kk, hd(a))
    mean = jnp.mean(y, axis=-1, keepdims=True)
    var = jnp.mean(jnp.square(y - mean), axis=-1, keepdims=True)
    y = ((y - mean) * lax.rsqrt(var + RW_GN_EPS)).reshape(bsz, t, RW_DIM)
    y = y * ln_w.astype(f32) + ln_b.astype(f32)
    bonus = jnp.sum(rh * kh * r_k.astype(f32), axis=-1, keepdims=True) * vh
    y = (y + bonus.reshape(bsz, t, RW_DIM)) * g
    return y, v_first


def _conv_ffn(h, w_up, conv_w, conv_b, w_down):
    t = h.shape[1]
    u = h @ w_up
    up = jnp.pad(u, ((0, 0), (CONV_W - 1, 0), (0, 0)))
    c = conv_b + up[:, 0:t] * conv_w[0]
    for j in range(1, CONV_W):
        c = c + up[:, j:j + t] * conv_w[j]
    gate, val = c[..., :D_FF], c[..., D_FF:]
    return (jax.nn.silu(gate) * val) @ w_down


def setup_inputs(seed: int = 0) -> dict:
    key = jax.random.key(seed)
    ks = iter(jax.random.split(key, 40))
    nrm = lambda shape, scale: jax.random.normal(next(ks), shape, jnp.float32) * scale
    gain = lambda shape: 1.0 + nrm(shape, 0.02)
    uni = lambda shape, lo, hi: jax.random.uniform(next(ks), shape, jnp.float32, lo, hi)
    L = DEPTH
    return {
        "x": nrm((BATCH, SEQ, D_MODEL), 1.0),
        "meta": nrm((N_META, D_MODEL), 1.0),
        "mix_norm": gain((L, D_MODEL)),
        "w_in": nrm((L, D_MODEL, W_IN), D_MODEL ** -0.5),
        "w_in_vres": nrm((L - 1, D_MODEL, RW_V_RANK), D_MODEL ** -0.5),
        "rw_mu": uni((L, RW_SHIFT), 0.0, 1.0),
        "rw_mu_vres": uni((L - 1, RW_V_RANK), 0.0, 1.0),
        "gla_gk_up": nrm((L, GLA_GATE_RANK, GLA_K), GLA_GATE_RANK ** -0.5),
        "gla_gk_bias": nrm((L, GLA_K), 0.1),
        "gla_norm": gain((L, GLA_DV)),
        "rw_w0": uni((L, RW_DIM), -6.0, -1.0),
        "rw_w2": nrm((L, RW_W_RANK, RW_DIM), 0.1),
        "rw_a0": nrm((L, RW_DIM), 0.1),
        "rw_a2": nrm((L, RW_A_RANK, RW_DIM), 0.1),
        "rw_v0": nrm((L - 1, RW_DIM), 0.1),
        "rw_v2": nrm((L - 1, RW_V_RANK, RW_DIM), 0.1),
        "rw_g2": nrm((L, RW_G_RANK, RW_DIM), RW_G_RANK ** -0.5),
        "rw_kk": 0.85 + nrm((L, RW_DIM), 0.05),
        "rw_ka": 1.0 + nrm((L, RW_DIM), 0.05),
        "rw_rk": nrm((L, RW_HEADS, RW_HD), 0.1),
        "rw_ln_w": gain((L, RW_DIM)),
        "rw_ln_b": nrm((L, RW_DIM), 0.02),
        "hg_lb_logits": nrm((L, HG_K), 0.1),
        "hg_norm": gain((L, HG_DV)),
        "w_out_gla": nrm((L, GLA_V, D_MODEL), GLA_V ** -0.5),
        "w_out_rw": nrm((L, RW_DIM, D_MODEL), RW_DIM ** -0.5),
        "w_out_hg": nrm((L, HG_V, D_MODEL), HG_V ** -0.5),
        "w_out": nrm((L, D_MODEL, D_MODEL), D_MODEL ** -0.5),
        "ffn_norm": gain((L, D_MODEL)),
        "w_up": nrm((L, D_MODEL, 2 * D_FF), D_MODEL ** -0.5),
        "conv_w": nrm((L, CONV_W, 2 * D_FF), CONV_W ** -0.5),
        "conv_b": nrm((L, 2 * D_FF), 0.02),
        "w_down": nrm((L, D_FF, D_MODEL), D_FF ** -0.5),
        "final_norm": gain((D_MODEL,)),
    }


def reference(x, meta, mix_norm, w_in, w_in_vres, rw_mu, rw_mu_vres, gla_gk_up, gla_gk_bias, gla_norm,
              rw_w0, rw_w2, rw_a0, rw_a2, rw_v0, rw_v2, rw_g2, rw_kk, rw_ka, rw_rk, rw_ln_w, rw_ln_b,
              hg_lb_logits, hg_norm, w_out_gla, w_out_rw, w_out_hg, w_out, ffn_norm, w_up, conv_w, conv_b,
              w_down, final_norm):
    bsz = x.shape[0]
    dt = x.dtype
    z = jnp.concatenate([jnp.broadcast_to(meta.astype(dt)[None], (bsz, N_META, D_MODEL)), x], axis=1)
    lb_p = jax.nn.softmax(hg_lb_logits.astype(jnp.float32), axis=0)
    lb_all = jnp.cumsum(lb_p, axis=0) - lb_p[0:1]
    v_first = None
    for i in range(DEPTH):
        h = _rmsnorm(z, mix_norm[i])
        if i == 0:
            w_cat, mu, rw_layout, v0_i, v2_i = w_in[0], rw_mu[0], RW_LAYOUT, None, None
        else:
            w_cat = jnp.concatenate([w_in[i], w_in_vres[i - 1]], axis=1)
            mu = jnp.concatenate([rw_mu[i], rw_mu_vres[i - 1]], axis=0)
            rw_layout, v0_i, v2_i = RW_LAYOUT + RW_VRES_LAYOUT, rw_v0[i - 1], rw_v2[i - 1]
        p = h @ w_cat
        f = _split(p[..., :NON_RW], NON_RW_LAYOUT)
        s = _split(_token_shift(p[..., NON_RW:], mu), rw_layout)
        y_gla = _gla_branch(f, gla_gk_up[i], gla_gk_bias[i], gla_norm[i]).astype(dt)
        y_rw, v_first = _rwkv7_branch(s, rw_w0[i], rw_w2[i], rw_a0[i], rw_a2[i], rw_g2[i], rw_kk[i], rw_ka[i],
                                      rw_rk[i], rw_ln_w[i], rw_ln_b[i], v_first, v0_i, v2_i)
        y_hg = _hgrn2_branch(f, lb_all[i], hg_norm[i]).astype(dt)
        merged = (jax.nn.sigmoid(f["gate_gla"]) * (y_gla @ w_out_gla[i])
                  + jax.nn.sigmoid(f["gate_rw"]) * (y_rw.astype(dt) @ w_out_rw[i])
                  + jax.nn.sigmoid(f["gate_hg"]) * (y_hg @ w_out_hg[i]))
        z = z + merged @ w_out[i]
        z = z + _conv_ffn(_rmsnorm(z, ffn_norm[i]), w_up[i], conv_w[i], conv_b[i], w_down[i])
    return _rmsnorm(z, final_norm)[:, N_META:]
```

```python
import numpy as np
from contextlib import ExitStack
import concourse.bass as bass
import concourse.mybir as mybir
from concourse.bass_utils import run_bass_kernel_spmd

F32 = mybir.dt.float32
BF16 = mybir.dt.bfloat16
AF = mybir.ActivationFunctionType
ALU = mybir.AluOpType

D = 1024
DEPTH = 2
N_META = 16
W_IN = 8464
D_FF = 2816
NFB = 22
SEM_LIMIT = 30000
EPS = 1e-6
RW_GN_EPS = 64e-5

C_GLA_QK, C_GLA_V, C_GLA_GKG = 0, 512, 1024
C_HG_Q, C_HG_F, C_HG_I, C_HG_G = 1552, 2064, 2576, 3088
C_GATE_GLA, C_GATE_RW, C_GATE_HG = 3600, 4624, 5648
C_RW = 6672

CI, CM1, CM3, CMSU, CMSL, CHO, CSEL_, CHS, CONE, CRST = 0, 128, 256, 384, 512, 640, 768, 772, 774, 902
NCC = 1030
PC_MU, PC_W0, PC_A0, PC_V0, PC_KK, PC_KA, PC_RK, PC_MIXN, PC_FFNN, PC_GLAN, PC_HGN, PC_LNW, PC_LNB = \
    0, 16, 20, 24, 28, 32, 36, 40, 48, 56, 57, 58, 62
PC_CW0, PC_CW1, PC_CW2, PC_CB = 66, 110, 154, 198
NPC = 242


class View:
    def __init__(self, b, ap):
        self.b = b
        self.ap = ap


class Buf:
    def __init__(self, t, name=""):
        self.t = t
        self.name = name
        self.w = None
        self.r = {}
        self.lane = None
        self.live = False

    def __getitem__(self, k):
        return View(self, self.t[k])


class _Alias:
    def __init__(self, parent, ap=None, pattern=None, a=None):
        self.b = parent
        if ap is None:
            ap = parent.t[:, :].rearrange(pattern, a=a)
        self.t = ap

    def __getitem__(self, k):
        return View(self.b, self.t[k])


class Lane:
    def __init__(self, S, name, step):
        self.S, self.name, self.step = S, name, step
        self.epoch, self.cnt = 0, 0
        self.sems = [S.new_sem(name + "_0")]

    def next_token(self):
        if self.cnt + self.step > SEM_LIMIT:
            self.epoch += 1
            self.cnt = 0
            self.sems.append(self.S.new_sem("%s_%d" % (self.name, self.epoch)))
        self.cnt += self.step
        return (self.name, self.epoch, self.cnt)


class Sched:
    def __init__(self, nc, stack):
        self.nc, self.stack = nc, stack
        self.lanes, self.h, self.seen = {}, {}, {}
        self.n_ins = 0
        self.n_wait = 0
        self.nsem = 0

    def new_sem(self, name):
        self.nsem += 1
        return self.stack.enter_context(self.nc.semaphore(name))

    def add_engine(self, name, handle):
        self.lanes[name] = Lane(self, name, 1)
        self.h[name] = handle
        self.seen[name] = {}

    def add_queue(self, name, handle):
        self.h[name] = handle
        self.seen[name] = {}

    def dma_lane(self, name):
        ln = Lane(self, name, 16)
        self.lanes[name] = ln
        return ln

    def _deps(self, eng, reads, writes):
        deps = []
        for b in reads:
            if b.w is not None:
                deps.append(b.w)
        for b in writes:
            if b.w is not None and b.w[0] != eng:
                deps.append(b.w)
            for en, tok in b.r.items():
                if en != eng:
                    deps.append(tok)
        if eng == "PE":
            deps = [d for d in deps if d[0] != "PE"]
        best = {}
        for d in deps:
            k = (d[0], d[1])
            if k not in best or best[k] < d[2]:
                best[k] = d[2]
        seen = self.seen[eng]
        h = self.h[eng]
        for (ln, ep), c in best.items():
            if seen.get((ln, ep), 0) >= c:
                continue
            h.wait_ge(self.lanes[ln].sems[ep], c)
            self.n_wait += 1
            seen[(ln, ep)] = c

    def op(self, eng, fn, reads=(), writes=()):
        self._deps(eng, reads, writes)
        ins = fn(self.h[eng])
        lane = self.lanes[eng]
        tok = lane.next_token()
        ins.then_inc(lane.sems[tok[1]], 1)
        self.n_ins += 1
        for b in reads:
            b.r[eng] = tok
        for b in writes:
            b.w = tok
            b.r = {}
        return tok

    def dma(self, queue, lane, fn, reads=(), writes=()):
        self._deps(queue, reads, writes)
        ins = fn(self.h[queue])
        tok = lane.next_token()
        ins.then_inc(lane.sems[tok[1]], 16)
        self.n_ins += 1
        for b in reads:
            b.r[lane.name] = tok
        for b in writes:
            b.w = tok
            b.r = {}
        return tok

    def wait_tok(self, eng, tok):
        self.h[eng].wait_ge(self.lanes[tok[0]].sems[tok[1]], tok[2])


class Ring:
    def __init__(self, bufs):
        self.bufs = bufs
        self.i = -1


class Builder:
    def __init__(self, nseq, nxg, depth=DEPTH, ntg=2, taps=()):
        self.nseq, self.nxg, self.depth, self.ntg = nseq, nxg, depth, ntg
        self.taps = set(taps)
        self.tap_names = []
        self.NTX = 128 * ntg
        self.SEQ = nxg * self.NTX

    def sb(self, name, shape, dt=F32):
        t = self.st.enter_context(self.nc.sbuf_tensor("s_" + name, list(shape), dt))
        return Buf(t, name)

    def rot(self, name, shape=None, dt=F32, n=2):
        if name not in self.rings:
            self.rings[name] = Ring([self.sb("%s_%d" % (name, i), shape, dt) for i in range(n)])
        r = self.rings[name]
        r.i = (r.i + 1) % len(r.bufs)
        return r.bufs[r.i]

    def bank(self, hold=False):
        for _ in range(8):
            self.pi = (self.pi + 1) % 8
            b = self.pbanks[self.pi]
            if not b.live:
                b.live = hold
                return b
        raise RuntimeError("no free psum bank")

    def lane_of(self, buf):
        if buf.lane is None:
            buf.lane = self.S.dma_lane("dl_" + buf.name)
        return buf.lane

    @staticmethod
    def _bufs(*vs):
        out = []
        for v in vs:
            if isinstance(v, View) and v.b not in out:
                out.append(v.b)
        return out

    @staticmethod
    def _a(v):
        return v.ap if isinstance(v, View) else v

    def mm(self, out, lhsT, rhs, start=True, stop=True):
        rd = self._bufs(lhsT, rhs) + ([] if start else [out.b])
        self.S.op("PE", lambda e: e.matmul(out.ap, lhsT.ap, rhs.ap, start=start, stop=stop),
                  reads=rd, writes=[out.b])

    def tr(self, out, in_):
        n = in_.ap.shape[0]
        ident = self.consts[0:n, CI:CI + n]
        self.S.op("PE", lambda e: e.transpose(out.ap, in_.ap, ident.ap),
                  reads=self._bufs(in_, ident), writes=[out.b])

    def act(self, out, in_, func, bias=0.0, scale=1.0, accum=None, eng="ACT"):
        kw = {}
        if accum is not None:
            kw["accum_out"] = accum.ap
        wr = self._bufs(out) + (self._bufs(accum) if accum is not None else [])
        self.S.op(eng, lambda e: e.activation(out=out.ap, in_=in_.ap, func=func, bias=self._a(bias),
                                              scale=self._a(scale), **kw),
                  reads=self._bufs(in_, bias, scale), writes=wr)

    def ts(self, eng, out, in0, s1, s2, op0, op1=None):
        if op1 is None:
            self.S.op(eng, lambda e: e.tensor_scalar(out.ap, in0.ap, self._a(s1), None, op0),
                      reads=self._bufs(in0, s1), writes=[out.b])
        else:
            self.S.op(eng, lambda e: e.tensor_scalar(out.ap, in0.ap, self._a(s1), self._a(s2), op0, op1),
                      reads=self._bufs(in0, s1, s2), writes=[out.b])

    def stt(self, out, in0, scalar, in1, op0, op1, eng="DVE"):
        self.S.op(eng, lambda e: e.scalar_tensor_tensor(out.ap, in0.ap, self._a(scalar), in1.ap, op0, op1),
                  reads=self._bufs(in0, scalar, in1), writes=[out.b])

    def tt(self, eng, out, in0, in1, op):
        self.S.op(eng, lambda e: e.tensor_tensor(out.ap, in0.ap, in1.ap, op),
                  reads=self._bufs(in0, in1), writes=[out.b])

    def cp(self, eng, out, in_):
        if eng == "ACT":
            self.S.op(eng, lambda e: e.copy(out.ap, in_.ap), reads=[in_.b], writes=[out.b])
        else:
            self.S.op(eng, lambda e: e.tensor_copy(out.ap, in_.ap), reads=[in_.b], writes=[out.b])

    def memset(self, eng, out, val):
        self.S.op(eng, lambda e: e.memset(out.ap, val), reads=[], writes=[out.b])

    def recip(self, out, in_):
        self.S.op("DVE", lambda e: e.reciprocal(out.ap, in_.ap), reads=[in_.b], writes=[out.b])

    def load(self, out, dram_ap, queue="SP"):
        return self.S.dma(queue, self.lane_of(out.b), lambda e: e.dma_start(out=out.ap, in_=dram_ap),
                          writes=[out.b])

    def tap(self, name, view, shape):
        if name not in self.taps:
            return
        nm = "tap_" + name
        if nm in self.tap_names:
            return
        self.tap_names.append(nm)
        d = self.nc.dram_tensor(nm, list(shape), view.ap.dtype, kind="ExternalOutput").ap()
        tb = Buf(None, nm)
        tok = self.S.dma("SP", self.lane_of(tb), lambda e: e.dma_start(out=d, in_=view.ap), reads=[view.b])
        self.final_toks.append(tok)

    def wsched_layer(self, l):
        seq = []
        win = self.d_w_in[l]

        def L(key, src, nkb, c0, n):
            seq.append((key, src, nkb, c0, n))
        L("gla_qk", win, 8, C_GLA_QK, 512)
        L("gla_v", win, 8, C_GLA_V, 512)
        L("gla_gkg", win, 8, C_GLA_GKG, 528)
        L("hg_q", win, 8, C_HG_Q, 512)
        L("hg_f", win, 8, C_HG_F, 512)
        L("hg_i", win, 8, C_HG_I, 512)
        L("hg_g", win, 8, C_HG_G, 512)
        L("rw_r", win, 8, C_RW, 512)
        L("rw_wk", win, 8, C_RW + 512, 576)
        L("rw_va", win, 8, C_RW + 1088, 576)
        L("rw_g", win, 8, C_RW + 1664, 128)
        if l >= 1:
            L("rw_vr", self.d_w_vres[l - 1], 8, 0, 32)
        for nm, c0, wo in (("gla", C_GATE_GLA, self.d_wo_gla), ("rw", C_GATE_RW, self.d_wo_rw),
                           ("hg", C_GATE_HG, self.d_wo_hg)):
            L("gate0_" + nm, win, 8, c0, 512)
            L("gate1_" + nm, win, 8, c0 + 512, 512)
            L("wo_" + nm, wo[l], 4, 0, 1024)
        L("wout0", self.d_w_out[l], 8, 0, 512)
        L("wout1", self.d_w_out[l], 8, 512, 512)
        for j in range(6):
            n = 512 if j < 5 else 256
            L("upg%d" % j, self.d_w_up[l], 8, j * 512, n)
            L("upv%d" % j, self.d_w_up[l], 8, D_FF + j * 512, n)
        for hf in range(2):
            for j in range(6):
                nk = 4 if j < 5 else 2
                L("dn%d_%d" % (hf, j), self.d_w_down[l][j * 512:j * 512 + nk * 128, :], nk, hf * 512, 512)
        return seq

    def wnext(self, key):
        i = self.wpos
        assert self.wseq[i][0] == key, (self.wseq[i][0], key)
        while self.wissued < min(len(self.wseq), i + self.LOOK + 1):
            j = self.wissued
            _, src, nkb, c0, n = self.wseq[j]
            slot = self.wslots[j % self.NSLOT]
            dst = View(slot, slot.t[:, 0:nkb * n].rearrange("p (k n) -> p k n", k=nkb))
            srcap = src.rearrange("(k p) n -> p k n", p=128)[:, :, c0:c0 + n]
            self.S.dma("POOL", self.lane_of(slot), lambda e, d=dst, s=srcap: e.dma_start(out=d.ap, in_=s),
                       writes=[slot])
            self.wissued += 1
        self.wpos += 1
        _, src, nkb, c0, n = self.wseq[i]
        slot = self.wslots[i % self.NSLOT]
        return View(slot, slot.t[:, 0:nkb * n].rearrange("p (k n) -> p k n", k=nkb))

    def proj_fm(self, wv, c0, m, hT, ntok, out):
        for kc in range(8):
            self.mm(out, View(wv.b, wv.ap[:, kc, c0:c0 + m]), View(hT.b, hT.ap[:, kc, 0:ntok]),
                    start=(kc == 0), stop=(kc == 7))

    def proj_tm(self, wv, c0, n, hT, t0, ts_, out):
        for kc in range(8):
            self.mm(out, View(hT.b, hT.ap[:, kc, t0:t0 + ts_]), View(wv.b, wv.ap[:, kc, c0:c0 + n]),
                    start=(kc == 0), stop=(kc == 7))

    def rms_rstd(self, zt, TS):
        sq = self.merged[-1]
        st = self.rot("rms_st", [128, 4], F32, n=3)
        self.act(sq[0:TS, :], zt[0:TS, :], AF.Square, accum=st[0:TS, 0:1])
        self.ts("DVE", st[0:TS, 1:2], st[0:TS, 0:1], 1.0 / D, EPS, ALU.mult, ALU.add)
        self.act(st[0:TS, 2:3], st[0:TS, 1:2], AF.Sqrt)
        self.recip(st[0:TS, 3:4], st[0:TS, 2:3])
        return st[0:TS, 3:4]

    def norm_T(self, l, pc_norm, hT, TS, nt):
        for t in range(nt):
            zt = self.z[t]
            rstd = self.rms_rstd(zt, TS)
            hn = self.merged[-1]
            self.act(hn[0:TS, :], zt[0:TS, :], AF.Identity, scale=rstd)
            for half in range(2):
                ps = self.bank()
                for q in range(4):
                    kc = half * 4 + q
                    self.tr(ps[0:128, q * TS:(q + 1) * TS], hn[0:TS, kc * 128:(kc + 1) * 128])
                for q in range(4):
                    kc = half * 4 + q
                    eng = "ACT" if q % 2 == 0 else "DVE"
                    dst = hT[:, kc, t * TS:(t + 1) * TS]
                    col = self.pcol[l][:, pc_norm + kc:pc_norm + kc + 1]
                    if eng == "ACT":
                        self.act(dst, ps[0:128, q * TS:(q + 1) * TS], AF.Identity, scale=col)
                    else:
                        self.ts("DVE", dst, ps[0:128, q * TS:(q + 1) * TS], col, None, ALU.mult)

    def chunk_gla_tile(self, tag, dk, qT, kT, ktm, vtm, glog, gs, qscale, St, gate_sg, yT, normcol, t, TS, nch):
        F = 4 * dk
        nb = F // 128
        hpb = 128 // dk
        W = 128 * hpb
        C = self.consts
        qts = self.rot("qts", [128, 4, 128], F32, n=1)
        kts = self.rot("kts", [128, 4, 128], F32, n=1)
        qhm = self.qhm
        dec = self.rot("dec", [128, 4, 4], F32, n=1)
        e3a = self.rot("e3a", [128, 4, 128], F32, n=1)
        for b in range(nb):
            ps = self.bank()
            gl = glog[0:TS, b * 128:(b + 1) * 128]
            self.mm(ps[0:128, 0:TS], gl, C[0:TS, CM1:CM1 + TS])
            self.mm(ps[0:128, 128:128 + TS], gl, C[0:TS, CM3:CM3 + TS])
            self.mm(ps[0:128, 256:256 + nch], gl, C[0:TS, CSEL_:CSEL_ + nch])
            e1 = self.rot("e1", [128, 128], F32, n=1)
            e2 = self.rot("e2", [128, 128], F32, n=1)
            self.act(e1[:, 0:TS], ps[0:128, 0:TS], AF.Exp, scale=gs)
            self.act(e2[:, 0:TS], ps[0:128, 0:TS], AF.Exp, scale=-gs)
            self.act(e3a[:, b, 0:TS], ps[0:128, 128:128 + TS], AF.Exp, scale=gs)
            self.act(dec[:, b, 0:nch], ps[0:128, 256:256 + nch], AF.Exp, scale=gs)
            qv = View(qT.b, qT.ap[:, b, :])
            kv = View(kT.b, kT.ap[:, b, :])
            self.stt(qts[:, b, 0:TS], qv, qscale, e1[:, 0:TS], ALU.mult, ALU.mult)
            self.tt("DVE", kts[:, b, 0:TS], kv, e2[:, 0:TS], ALU.mult)
        ps4 = self.bank()
        self.mm(ps4[0:TS, 0:F], C[0:TS, CMSL:CMSL + TS], glog[0:TS, 0:F])
        e4 = self.rot("A6", [128, 512], F32, n=1)
        self.act(e4[0:TS, 0:F], ps4[0:TS, 0:F], AF.Exp, scale=gs)
        khat = self.rot("A7", [128, 512], F32, n=1)
        self.tt("DVE", khat[0:TS, 0:F], ktm[0:TS, 0:F], e4[0:TS, 0:F], ALU.mult)
        khm_b = self.rot("B0", [128, 2048], F32, n=1)
        khm = View(khm_b, khm_b.t[:, :].rearrange("p (c f) -> p c f", c=4))
        for c in range(nch):
            self.ts("POOL", View(khm_b, khm.ap[0:TS, c, 0:F]), khat[0:TS, 0:F], C[0:TS, CSEL_ + c:CSEL_ + c + 1], None, ALU.mult)
        snaps = [St] + [self.rot("sn%d" % i, [128, 512], F32, n=1) for i in range(nch - 1)]

        def upd(c):
            for b in range(nb):
                psi = self.bank()
                self.mm(psi[0:128, 0:W], View(khm_b, khm.ap[0:TS, c, b * 128:(b + 1) * 128]), vtm[0:TS, b * W:(b + 1) * W])
                dst = snaps[c + 1] if c + 1 < nch else St
                self.stt(dst[:, b * W:(b + 1) * W], snaps[c][:, b * W:(b + 1) * W], dec[:, b, c:c + 1], psi[0:128, 0:W], ALU.mult, ALU.add)
        for c in range(nch - 1):
            upd(c)
        pso = self.bank(hold=True)
        for h in range(4):
            b = (h * dk) // 128
            po = (h * dk) % 128
            hb = h % hpb
            if po == 0:
                for c in range(nch):
                    self.stt(qhm[:, c, c * 32:(c + 1) * 32], View(qT.b, qT.ap[:, b, c * 32:(c + 1) * 32]), qscale,
                             e3a[:, b, c * 32:(c + 1) * 32], ALU.mult, ALU.mult)
            pss = self.bank()
            self.mm(pss[0:TS, 0:TS], kts[po:po + dk, b, 0:TS], qts[po:po + dk, b, 0:TS])
            pt = self.rot("pt", [128, 128], F32, n=2)
            self.tt("DVE", pt[0:TS, 0:TS], pss[0:TS, 0:TS], C[0:TS, CM3:CM3 + TS], ALU.mult)
            oh = pso[0:TS, h * 128:(h + 1) * 128]
            self.mm(oh, pt[0:TS, 0:TS], vtm[0:TS, h * 128:(h + 1) * 128], start=True, stop=False)
            for c in range(nch):
                self.mm(oh, qhm[po:po + dk, c, 0:TS], snaps[c][po:po + dk, b * W + hb * 128:b * W + (hb + 1) * 128],
                        start=False, stop=(c == nch - 1))
        upd(nch - 1)
        st = self.rot("hn_st", [128, 16], F32, n=1)
        junk = self.rot("hn_junk", [128, 128], F32, n=1)
        for h in range(4):
            self.act(junk[0:TS, :], pso[0:TS, h * 128:(h + 1) * 128], AF.Square, accum=st[0:TS, h:h + 1])
        self.ts("DVE", st[0:TS, 4:8], st[0:TS, 0:4], 1.0 / 128, EPS, ALU.mult, ALU.add)
        self.act(st[0:TS, 8:12], st[0:TS, 4:8], AF.Sqrt)
        self.recip(st[0:TS, 12:16], st[0:TS, 8:12])
        ytm = self.rot("A8", [128, 512], F32, n=1)
        for h in range(4):
            self.stt(ytm[0:TS, h * 128:(h + 1) * 128], pso[0:TS, h * 128:(h + 1) * 128], st[0:TS, 12 + h:13 + h],
                     gate_sg[0:TS, h * 128:(h + 1) * 128], ALU.mult, ALU.mult)
        pso.live = False
        pst = self.bank()
        for h in range(4):
            self.tr(pst[0:128, h * TS:(h + 1) * TS], ytm[0:TS, h * 128:(h + 1) * 128])
        for h in range(4):
            self.act(yT[:, h, t * TS:(t + 1) * TS], pst[0:128, h * TS:(h + 1) * TS], AF.Identity, scale=normcol)

    def gla_branch(self, l, hT, TS, nt, nch):
        NT = TS * nt
        C = self.consts
        w_qk = self.wnext("gla_qk")
        w_v = self.wnext("gla_v")
        w_g = self.wnext("gla_gkg")
        qT = self.qTb
        kT = self.kTb
        for t in range(nt):
            t0 = t * TS
            hTt = View(hT.b, hT.ap[:, :, t0:t0 + TS])
            for b in range(2):
                ps = self.bank()
                self.proj_fm(w_qk, b * 128, 128, hTt, TS, ps[0:128, 0:TS])
                self.cp("ACT", qT[:, b, 0:TS], ps[0:128, 0:TS])
                ps = self.bank()
                self.proj_fm(w_qk, 256 + b * 128, 128, hTt, TS, ps[0:128, 0:TS])
                self.cp("DVE", kT[:, b, 0:TS], ps[0:128, 0:TS])
            gkT = self.rot("A5", [128, 512], F32, n=1)
            ps = self.bank()
            self.proj_fm(w_g, 0, 16, hTt, TS, ps[0:16, 0:TS])
            self.cp("ACT", gkT[0:16, 0:TS], ps[0:16, 0:TS])
            ps = self.bank()
            for b in range(2):
                self.tr(ps[0:TS, b * 128:(b + 1) * 128], kT[:, b, 0:TS])
            ktm = self.rot("A0", [128, 512], F32, n=1)
            self.cp("ACT", ktm[0:TS, 0:256], ps[0:TS, 0:256])
            ps = self.bank()
            self.proj_tm(w_v, 0, 512, hT, t0, TS, ps[0:TS, 0:512])
            vtm = self.rot("A1", [128, 512], F32, n=1)
            self.cp("DVE", vtm[0:TS, :], ps[0:TS, 0:512])
            ps = self.bank()
            self.proj_tm(w_g, 16, 512, hT, t0, TS, ps[0:TS, 0:512])
            sg = self.rot("A2", [128, 512], F32, n=1)
            self.act(sg[0:TS, :], ps[0:TS, 0:512], AF.Silu)
            ps = self.bank()
            self.mm(ps[0:TS, 0:256], gkT[0:16, 0:TS], self.gkup[l][0:16, :], start=True, stop=False)
            self.mm(ps[0:TS, 0:256], C[0:1, CONE:CONE + TS], self.gkb[l][0:1, :], start=False, stop=True)
            glog = self.rot("A3", [128, 512], F32, n=1)
            self.act(glog[0:TS, 0:256], ps[0:TS, 0:256], AF.Exp, scale=-1.0)
            self.act(glog[0:TS, 0:256], glog[0:TS, 0:256], AF.Ln, bias=1.0)
            self.chunk_gla_tile("gla", 64, View(qT, qT.t[:, :, 0:TS]), View(kT, kT.t[:, :, 0:TS]),
                                ktm, vtm, glog, -1.0 / 16.0, 0.125, self.Sgla[l], sg, self.yT_gla,
                                self.pcol[l][:, PC_GLAN:PC_GLAN + 1], t, TS, nch)

    def hg_branch(self, l, hT, TS, nt, nch):
        NT = TS * nt
        w_q = self.wnext("hg_q")
        w_f = self.wnext("hg_f")
        w_i = self.wnext("hg_i")
        w_g = self.wnext("hg_g")
        qT = self.qTb
        kT = self.kTb
        for t in range(nt):
            t0 = t * TS
            hTt = View(hT.b, hT.ap[:, :, t0:t0 + TS])
            for b in range(4):
                ps = self.bank()
                self.proj_fm(w_q, b * 128, 128, hTt, TS, ps[0:128, 0:TS])
                self.cp("ACT" if b % 2 == 0 else "DVE", qT[:, b, 0:TS], ps[0:128, 0:TS])
            ps = self.bank()
            self.proj_tm(w_f, 0, 512, hT, t0, TS, ps[0:TS, 0:512])
            sgf = self.rot("A4", [128, 512], F32, n=1)
            self.act(sgf[0:TS, :], ps[0:TS, 0:512], AF.Sigmoid)
            ktm = self.rot("A0", [128, 512], F32, n=1)
            glog = self.rot("A3", [128, 512], F32, n=1)
            if l == 0:
                self.ts("DVE", ktm[0:TS, :], sgf[0:TS, :], -1.0, 1.0, ALU.mult, ALU.add)
                self.ts("POOL", sgf[0:TS, :], sgf[0:TS, :], 1e-30, None, ALU.max)
                self.act(glog[0:TS, :], sgf[0:TS, :], AF.Ln)
            else:
                tmp = self.rot("A5", [128, 512], F32, n=1)
                self.tt("DVE", tmp[0:TS, :], sgf[0:TS, :], self.oml[0:TS, :], ALU.mult)
                self.tt("DVE", ktm[0:TS, :], self.oml[0:TS, :], tmp[0:TS, :], ALU.subtract)
                self.stt(tmp[0:TS, :], tmp[0:TS, :], 1e-30, self.lb[0:TS, :], ALU.max, ALU.add)
                self.act(glog[0:TS, :], tmp[0:TS, :], AF.Ln)
            ps = self.bank()
            for b in range(4):
                self.tr(ps[0:128, b * TS:(b + 1) * TS], ktm[0:TS, b * 128:(b + 1) * 128])
            for b in range(4):
                self.cp("ACT" if b % 2 == 0 else "DVE", kT[:, b, 0:TS], ps[0:128, b * TS:(b + 1) * TS])
            ps = self.bank()
            self.proj_tm(w_i, 0, 512, hT, t0, TS, ps[0:TS, 0:512])
            vtm = self.rot("A1", [128, 512], F32, n=1)
            self.cp("DVE", vtm[0:TS, :], ps[0:TS, 0:512])
            ps = self.bank()
            self.proj_tm(w_g, 0, 512, hT, t0, TS, ps[0:TS, 0:512])
            sg = self.rot("A2", [128, 512], F32, n=1)
            self.act(sg[0:TS, :], ps[0:TS, 0:512], AF.Silu)
            self.chunk_gla_tile("hg", 128, View(qT, qT.t[:, :, 0:TS]), View(kT, kT.t[:, :, 0:TS]),
                                ktm, vtm, glog, 1.0, 1.0, self.Shg[l], sg, self.yT_hg,
                                self.pcol[l][:, PC_HGN:PC_HGN + 1], t, TS, nch)

    def rw_branch(self, l, hT, TS, nt, nch):
        C = self.consts
        pc = self.pcol[l]
        w_r = self.wnext("rw_r")
        w_wk = self.wnext("rw_wk")
        w_va = self.wnext("rw_va")
        w_g = self.wnext("rw_g")
        w_vr = self.wnext("rw_vr") if l >= 1 else None
        pw = self.prw
        H = self.Hrw[l]
        blocks = []
        for b in range(4):
            blocks.append((w_r, b * 128, 128))
        blocks.append((w_wk, 0, 64))
        for b in range(4):
            blocks.append((w_wk, 64 + b * 128, 128))
        for b in range(4):
            blocks.append((w_va, b * 128, 128))
        blocks.append((w_va, 512, 64))
        blocks.append((w_g, 0, 128))
        if l >= 1:
            blocks.append((w_vr, 0, 32))
        nblk = len(blocks)
        for t in range(nt):
            t0 = t * TS
            self.cp("POOL", View(pw, pw.t[:, :, 0]), self.crw[l][:, 0:16])
            for j, (wv, c0, m) in enumerate(blocks):
                ps = self.bank()
                self.proj_fm(wv, c0, m, View(hT.b, hT.ap[:, :, t0:t0 + TS]), TS, ps[0:m, 0:TS])
                self.cp("ACT" if j % 2 == 0 else "DVE", pw[0:m, j, 1:1 + TS], ps[0:m, 0:TS])
            self.cp("POOL", self.crw[l][:, 0:16], View(pw, pw.t[:, :, TS]))
            s_b = self.rot("B0", [128, 2048], F32, n=1)
            s = _Alias(s_b, s_b.t[:, :].rearrange("p (j t) -> p j t", j=16))
            self.tt("DVE", s[:, :, 0:TS], pw[:, :, 0:TS], pw[:, :, 1:1 + TS], ALU.subtract)
            mub = View(pc, pc.t[:, PC_MU:PC_MU + 16].unsqueeze(2).to_broadcast([128, 16, TS]))
            self.tt("POOL", s[:, :, 0:TS], s[:, :, 0:TS], mub, ALU.mult)
            self.tt("DVE", s[:, :, 0:TS], s[:, :, 0:TS], pw[:, :, 1:1 + TS], ALU.add)
            tw = self.rot("tw", [128, 128], F32, n=1)
            self.act(tw[0:64, 0:TS], s[0:64, 4, 0:TS], AF.Tanh)
            sgs = self.rot("sgs", [128, 128], F32, n=1)
            self.act(sgs[:, 0:TS], s[:, 14, 0:TS], AF.Sigmoid)
            for b in range(4):
                self.rw_block(l, b, s, tw, sgs, H, t, TS, nch)

    def rw_block(self, l, b, s, tw, sgs, H, t, TS, nch):
        C = self.consts
        pc = self.pcol[l]
        t0 = t * TS

        def col(base):
            return pc[:, base + b:base + b + 1]

        def q(name, n=1):
            return self.rot("rw_" + name, [128, 128], F32, n=n)
        sr = s[:, b, 0:TS]
        sk = s[:, 5 + b, 0:TS]
        sv = s[:, 9 + b, 0:TS]
        ps = self.bank()
        self.mm(ps[0:128, 0:TS], self.w2[l][0:64, b * 128:(b + 1) * 128], tw[0:64, 0:TS])
        lw = q("lw")
        self.act(lw[:, 0:TS], ps[0:128, 0:TS], AF.Sigmoid, bias=col(PC_W0))
        self.ts("POOL", lw[:, 0:TS], lw[:, 0:TS], -0.6065306597126334, None, ALU.mult)
        ps = self.bank()
        self.mm(ps[0:128, 0:TS], self.a2[l][0:64, b * 128:(b + 1) * 128], s[0:64, 13, 0:TS])
        a = q("a")
        self.act(a[:, 0:TS], ps[0:128, 0:TS], AF.Sigmoid, bias=col(PC_A0))
        vf = self.vfirst[:, b, t0:t0 + TS]
        if l == 0:
            self.cp("POOL", vf, sv)
            v = vf
        else:
            ps = self.bank()
            self.mm(ps[0:128, 0:TS], self.v2[0:32, b * 128:(b + 1) * 128], s[0:32, 15, 0:TS])
            vg = q("vg")
            self.act(vg[:, 0:TS], ps[0:128, 0:TS], AF.Sigmoid, bias=col(PC_V0))
            vt = q("v")
            self.tt("DVE", vt[:, 0:TS], vf, sv, ALU.subtract)
            self.tt("DVE", vt[:, 0:TS], vt[:, 0:TS], vg[:, 0:TS], ALU.mult)
            self.tt("DVE", vt[:, 0:TS], vt[:, 0:TS], sv, ALU.add)
            v = vt[:, 0:TS]
        kk = q("kk")
        self.ts("DVE", kk[:, 0:TS], sk, col(PC_KK), None, ALU.mult)
        sq = q("sq")
        self.tt("POOL", sq[:, 0:TS], kk[:, 0:TS], kk[:, 0:TS], ALU.mult)
        ps = self.bank()
        self.mm(ps[0:128, 0:TS], C[:, CHO:CHO + 128], sq[:, 0:TS])
        self.act(sq[:, 0:TS], ps[0:128, 0:TS], AF.Sqrt)
        self.ts("DVE", sq[:, 0:TS], sq[:, 0:TS], 1e-12, None, ALU.max)
        self.recip(sq[:, 0:TS], sq[:, 0:TS])
        self.tt("DVE", kk[:, 0:TS], kk[:, 0:TS], sq[:, 0:TS], ALU.mult)
        km = q("km")
        self.ts("DVE", km[:, 0:TS], a[:, 0:TS], 1.0, col(PC_KA), ALU.subtract, ALU.mult)
        self.stt(km[:, 0:TS], km[:, 0:TS], 1.0, sk, ALU.add, ALU.mult)
        beta = q("beta")
        self.tt("POOL", beta[:, 0:TS], kk[:, 0:TS], a[:, 0:TS], ALU.mult)
        cl = q("cl")
        self.S.op("DVE", lambda e: e.tensor_tensor_scan(cl[:, 0:TS].ap, C[:, CRST:CRST + TS].ap, lw[:, 0:TS].ap,
                                                        0.0, ALU.mult, ALU.add),
                  reads=[C, lw], writes=[cl])

        def v3(buf):
            return buf.t[:, 0:TS].rearrange("p (c i) -> p c i", i=32)
        cl3 = v3(cl)
        ref_b = View(cl, cl3[:, :, 15:16].to_broadcast([128, nch, 32]))
        last_b = View(cl, cl3[:, :, 31:32].to_broadcast([128, nch, 32]))
        dr = q("dr")
        self.tt("DVE", View(dr, v3(dr)), View(cl, cl3), ref_b, ALU.subtract)
        er = q("er")
        self.act(er[:, 0:TS], dr[:, 0:TS], AF.Exp)
        einv = q("einv")
        self.act(einv[:, 0:TS], dr[:, 0:TS], AF.Exp, scale=-1.0)
        ea = q("ea")
        self.tt("POOL", ea[:, 0:TS], dr[:, 0:TS], lw[:, 0:TS], ALU.subtract)
        self.act(ea[:, 0:TS], ea[:, 0:TS], AF.Exp)
        el = q("el")
        self.tt("DVE", View(el, v3(el)), last_b, View(cl, cl3), ALU.subtract)
        self.act(el[:, 0:TS], el[:, 0:TS], AF.Exp)
        ea0 = q("ea0")
        self.tt("POOL", ea0[:, 0:TS], cl[:, 0:TS], lw[:, 0:TS], ALU.subtract)
        self.act(ea0[:, 0:TS], ea0[:, 0:TS], AF.Exp)
        er0 = q("er0")
        self.act(er0[:, 0:TS], cl[:, 0:TS], AF.Exp)
        gam = q("gam")
        self.act(View(gam, gam.t[:, 0:nch]), View(cl, cl3[:, :, 31]), AF.Exp)
        abar = ea
        self.stt(abar[:, 0:TS], kk[:, 0:TS], -1.0, ea[:, 0:TS], ALU.mult, ALU.mult)
        rbar = er
        self.tt("DVE", rbar[:, 0:TS], sr, er[:, 0:TS], ALU.mult)
        btil = q("btil")
        self.tt("DVE", btil[:, 0:TS], beta[:, 0:TS], einv[:, 0:TS], ALU.mult)
        ktil = einv
        self.tt("DVE", ktil[:, 0:TS], km[:, 0:TS], einv[:, 0:TS], ALU.mult)
        f4 = _Alias(self.rot("A0", [128, 512], F32, n=1), None, "p (a b) -> p a b", 4)
        self.tt("DVE", f4[:, 0, 0:TS], beta[:, 0:TS], el[:, 0:TS], ALU.mult)
        self.tt("DVE", f4[:, 1, 0:TS], km[:, 0:TS], el[:, 0:TS], ALU.mult)
        self.stt(f4[:, 2, 0:TS], kk[:, 0:TS], -1.0, ea0[:, 0:TS], ALU.mult, ALU.mult)
        self.cp("POOL", f4[:, 3, 0:TS], v)
        rb0m = self.rb0m
        for c in range(nch):
            self.tt("DVE", rb0m[:, c, c * 32:(c + 1) * 32], View(sr.b, s.t[:, b, c * 32:(c + 1) * 32]),
                    er0[:, c * 32:(c + 1) * 32], ALU.mult)
        rkr = q("rkr")
        self.stt(rkr[:, 0:TS], sr, col(PC_RK), km[:, 0:TS], ALU.mult, ALU.mult)
        psbon = self.bank(hold=True)
        self.mm(psbon[0:128, 0:TS], C[:, CHO:CHO + 128], rkr[:, 0:TS])
        bon = q("bon")
        self.tt("DVE", bon[:, 0:TS], psbon[0:128, 0:TS], v, ALU.mult)
        psbon.live = False
        ps = self.bank()
        for j in range(4):
            self.tr(ps[0:TS, j * 128:(j + 1) * 128], f4[:, j, 0:TS])
        tm4 = _Alias(self.rot("A1", [128, 512], F32, n=1), None, "p (a b) -> p a b", 4)
        self.cp("ACT", View(tm4.b, tm4.b.t[0:TS, :]), ps[0:TS, 0:512])
        bh_tm = tm4[0:TS, 0, :]
        kh_tm = tm4[0:TS, 1, :]
        vm = _Alias(self.rot("A2", [128, 512], F32, n=1), None, "p (a b) -> p a b", 4)
        for c in range(nch):
            self.ts("POOL", vm[0:TS, c, :], tm4[0:TS, 3, :], C[0:TS, CSEL_ + c:CSEL_ + c + 1], None, ALU.mult)
        ww = self.rot("rw_ww", [128, 128], F32, n=1)
        u0p = self.rot("rw_u0p", [128, 128], F32, n=1)
        arbT = []
        arkT = []
        for hh in range(2):
            po = 64 * hh
            A_ = abar[po:po + 64, 0:TS]
            B_ = btil[po:po + 64, 0:TS]
            K_ = ktil[po:po + 64, 0:TS]
            R_ = rbar[po:po + 64, 0:TS]
            ps = self.bank()
            self.mm(ps[0:TS, 0:TS], B_, A_)
            self.mm(ps[0:TS, 128:128 + TS], A_, B_)
            self.mm(ps[0:TS, 256:256 + TS], K_, A_)
            nT = self.rot("rw_nT", [128, 128], F32, n=2)
            nN = self.rot("rw_nN", [128, 128], F32, n=2)
            aak = self.rot("rw_aak", [128, 128], F32, n=1)
            self.tt("DVE", nT[0:TS, 0:TS], ps[0:TS, 0:TS], C[0:TS, CMSU:CMSU + TS], ALU.mult)
            self.tt("DVE", nN[0:TS, 0:TS], ps[0:TS, 128:128 + TS], C[0:TS, CMSL:CMSL + TS], ALU.mult)
            self.tt("DVE", aak[0:TS, 0:TS], ps[0:TS, 256:256 + TS], C[0:TS, CMSU:CMSU + TS], ALU.mult)
            ps = self.bank()
            self.mm(ps[0:TS, 0:TS], B_, R_)
            self.mm(ps[0:TS, 128:128 + TS], K_, R_)
            rb = self.rot("rw_arb%d" % hh, [128, 2, 128], F32, n=1)
            self.tt("DVE", rb[0:TS, 0, 0:TS], ps[0:TS, 0:TS], C[0:TS, CM3:CM3 + TS], ALU.mult)
            self.tt("DVE", rb[0:TS, 1, 0:TS], ps[0:TS, 128:128 + TS], C[0:TS, CM3:CM3 + TS], ALU.mult)
            arbT.append(rb[0:TS, 0, 0:TS])
            arkT.append(rb[0:TS, 1, 0:TS])
            ps = self.bank()
            self.mm(ps[0:TS, 0:64], aak[0:TS, 0:TS], tm4[0:TS, 3, po:po + 64])
            x = self.rot("rw_x", [128, 128], F32, n=3)
            self.cp("POOL", x[0:TS, 0:64], tm4[0:TS, 2, po:po + 64])
            self.cp("ACT", x[0:TS, 64:128], ps[0:TS, 0:64])
            for lv in range(5):
                ps = self.bank()
                self.mm(ps[0:TS, 0:128], nT[0:TS, 0:TS], x[0:TS, 0:128])
                if lv < 4:
                    ps2 = self.bank()
                    self.mm(ps2[0:TS, 0:TS], nT[0:TS, 0:TS], nN[0:TS, 0:TS])
                    self.mm(ps2[0:TS, 128:128 + TS], nN[0:TS, 0:TS], nT[0:TS, 0:TS])
                    nN = self.rot("rw_nN", [128, 128], F32, n=2)
                    nT = self.rot("rw_nT", [128, 128], F32, n=2)
                    self.cp("ACT", nN[0:TS, 0:TS], ps2[0:TS, 0:TS])
                    self.cp("ACT", nT[0:TS, 0:TS], ps2[0:TS, 128:128 + TS])
                    xn = self.rot("rw_x", [128, 128], F32, n=3)
                    self.tt("DVE", xn[0:TS, :], ps[0:TS, 0:128], x[0:TS, :], ALU.add)
                    x = xn
                else:
                    self.tt("DVE", ww[0:TS, po:po + 64], ps[0:TS, 0:64], x[0:TS, 0:64], ALU.add)
                    self.tt("DVE", u0p[0:TS, po:po + 64], ps[0:TS, 64:128], x[0:TS, 64:128], ALU.add)
        ps = self.bank()
        self.tr(ps[0:128, 0:TS], ww[0:TS, 0:128])
        wta = self.rot("rw_wta", [128, 128], F32, n=1)
        wtb = self.rot("rw_wtb", [128, 128], F32, n=1)
        self.cp("ACT", wta[0:64, 0:TS], ps[0:64, 0:TS])
        self.cp("DVE", wtb[64:128, 0:TS], ps[64:128, 0:TS])
        u0m = _Alias(self.rot("A3", [128, 512], F32, n=1), None, "p (a b) -> p a b", 4)
        for c in range(nch):
            self.ts("POOL", u0m[0:TS, c, :], u0p[0:TS, :], C[0:TS, CSEL_ + c:CSEL_ + c + 1], None, ALU.mult)
        ya = self.bank(hold=True)
        yb = self.bank(hold=True)
        ys = (ya, yb)
        for c in range(nch):
            psu = self.bank()
            self.mm(psu[0:TS, 0:64], wta[:, 0:TS], H[:, b, 0:64])
            self.mm(psu[0:TS, 64:128], wtb[:, 0:TS], H[:, b, 64:128])
            ucm = self.rot("rw_ucm", [128, 128], F32, n=2)
            self.stt(ucm[0:TS, :], psu[0:TS, 0:128], C[0:TS, CSEL_ + c:CSEL_ + c + 1], u0m[0:TS, c, :], ALU.mult, ALU.add)
            for hh in range(2):
                po = 64 * hh
                self.mm(ys[hh][0:TS, 0:64], rb0m[po:po + 64, c, 0:TS], H[po:po + 64, b, po:po + 64],
                        start=(c == 0), stop=False)
                self.mm(ys[hh][0:TS, 0:64], arbT[hh], ucm[0:TS, po:po + 64], start=False, stop=False)
            psh = self.bank()
            self.mm(psh[0:128, 0:128], bh_tm, ucm[0:TS, :], start=True, stop=False)
            self.mm(psh[0:128, 0:128], kh_tm, vm[0:TS, c, :], start=False, stop=True)
            self.stt(H[:, b, :], H[:, b, :], View(gam, gam.t[:, c:c + 1]), psh[0:128, 0:128], ALU.mult, ALU.add)
        for hh in range(2):
            po = 64 * hh
            self.mm(ys[hh][0:TS, 0:64], arkT[hh], tm4[0:TS, 3, po:po + 64], start=False, stop=True)
        st = self.rot("rw_st", [128, 12], F32, n=1)
        junk = self.rot("hn_junk", [128, 128], F32, n=1)
        for hh in range(2):
            self.act(junk[0:TS, 0:64], ys[hh][0:TS, 0:64], AF.Identity, accum=st[0:TS, hh:hh + 1])
            self.act(junk[0:TS, 64:128], ys[hh][0:TS, 0:64], AF.Square, accum=st[0:TS, 2 + hh:3 + hh])
        self.ts("DVE", st[0:TS, 4:6], st[0:TS, 0:2], 1.0 / 64, None, ALU.mult)
        self.tt("DVE", st[0:TS, 6:8], st[0:TS, 4:6], st[0:TS, 4:6], ALU.mult)
        self.stt(st[0:TS, 8:10], st[0:TS, 2:4], 1.0 / 64, st[0:TS, 6:8], ALU.mult, ALU.subtract)
        self.ts("DVE", st[0:TS, 8:10], st[0:TS, 8:10], RW_GN_EPS, None, ALU.add)
        self.act(st[0:TS, 8:10], st[0:TS, 8:10], AF.Sqrt)
        self.recip(st[0:TS, 10:12], st[0:TS, 8:10])
        yn = self.rot("rw_yn", [128, 128], F32, n=1)
        for hh in range(2):
            po = 64 * hh
            self.ts("DVE", yn[0:TS, po:po + 64], ys[hh][0:TS, 0:64], st[0:TS, 4 + hh:5 + hh], st[0:TS, 10 + hh:11 + hh],
                    ALU.subtract, ALU.mult)
        ya.live = False
        yb.live = False
        ps = self.bank()
        self.tr(ps[0:128, 0:TS], yn[0:TS, 0:128])
        y1 = q("y1")
        self.act(y1[:, 0:TS], ps[0:128, 0:TS], AF.Identity, scale=col(PC_LNW), bias=col(PC_LNB))
        self.tt("DVE", y1[:, 0:TS], y1[:, 0:TS], bon[:, 0:TS], ALU.add)
        ps = self.bank()
        self.mm(ps[0:128, 0:TS], self.g2[l][:, b * 128:(b + 1) * 128], sgs[:, 0:TS])
        self.tt("DVE", self.yT_rw[:, b, t0:t0 + TS], y1[:, 0:TS], ps[0:128, 0:TS], ALU.mult)

    def merge(self, l, hT, TS, nt):
        NT = TS * nt
        for bi, (nm, yT) in enumerate((("gla", self.yT_gla), ("rw", self.yT_rw), ("hg", self.yT_hg))):
            g0 = self.wnext("gate0_" + nm)
            g1 = self.wnext("gate1_" + nm)
            wo = self.wnext("wo_" + nm)
            for t in range(nt):
                t0 = t * TS
                for hf in range(2):
                    ps = self.bank()
                    self.proj_tm(g0 if hf == 0 else g1, 0, 512, hT, t0, TS, ps[0:TS, 0:512])
                    sig = self.rot("A2", [128, 512], F32, n=1)
                    self.act(sig[0:TS, :], ps[0:TS, 0:512], AF.Sigmoid)
                    ps = self.bank()
                    for kb in range(4):
                        self.mm(ps[0:TS, 0:512], yT[:, kb, t0:t0 + TS], View(wo.b, wo.ap[:, kb, hf * 512:(hf + 1) * 512]),
                                start=(kb == 0), stop=(kb == 3))
                    mg = self.merged[t][0:TS, hf * 512:(hf + 1) * 512]
                    if bi == 0:
                        self.tt("DVE", mg, sig[0:TS, :], ps[0:TS, 0:512], ALU.mult)
                    else:
                        self.tt("DVE", sig[0:TS, :], sig[0:TS, :], ps[0:TS, 0:512], ALU.mult)
                        self.tt("POOL", mg, mg, sig[0:TS, :], ALU.add)
        mT = self.mT
        for t in range(nt):
            for half in range(2):
                ps = self.bank()
                for q in range(4):
                    kc = half * 4 + q
                    self.tr(ps[0:128, q * TS:(q + 1) * TS], self.merged[t][0:TS, kc * 128:(kc + 1) * 128])
                for q in range(4):
                    kc = half * 4 + q
                    self.cp("ACT" if q % 2 == 0 else "DVE", mT[:, kc, t * TS:(t + 1) * TS], ps[0:128, q * TS:(q + 1) * TS])
        for hf in range(2):
            wv = self.wnext("wout%d" % hf)
            for t in range(nt):
                ps = self.bank()
                for kc in range(8):
                    self.mm(ps[0:TS, 0:512], mT[:, kc, t * TS:(t + 1) * TS], View(wv.b, wv.ap[:, kc, :]),
                            start=(kc == 0), stop=(kc == 7))
                zv = self.z[t][0:TS, hf * 512:(hf + 1) * 512]
                self.tt("DVE", zv, zv, ps[0:TS, 0:512], ALU.add)

    def ffn(self, l, hT, TS, nt):
        NT = TS * nt
        pc = self.pcol[l]
        actT = self.actT
        uc = self.uc[l]
        for j in range(6):
            nb = 4 if j < 5 else 2
            wg = self.wnext("upg%d" % j)
            wv = self.wnext("upv%d" % j)
            for bb in range(nb):
                fb = j * 4 + bb
                cs = []
                for which, w_ in ((0, wg), (1, wv)):
                    ci = which * NFB + fb
                    ps = self.bank()
                    self.proj_fm(w_, bb * 128, 128, hT, NT, ps[0:128, 0:NT])
                    ub = self.rot("A%d" % which, [128, 512], F32, n=1)
                    self.cp("POOL", ub[:, 0:2], uc[:, ci, :])
                    self.cp("ACT", ub[:, 2:2 + NT], ps[0:128, 0:NT])
                    self.cp("POOL", uc[:, ci, :], ub[:, NT:NT + 2])
                    cg = self.rot("A%d" % (3 + which), [128, 512], F32, n=1)
                    self.ts("DVE", cg[:, 0:NT], ub[:, 2:2 + NT], pc[:, PC_CW2 + ci:PC_CW2 + ci + 1],
                            pc[:, PC_CB + ci:PC_CB + ci + 1], ALU.mult, ALU.add)
                    self.stt(cg[:, 0:NT], ub[:, 1:1 + NT], pc[:, PC_CW1 + ci:PC_CW1 + ci + 1], cg[:, 0:NT], ALU.mult, ALU.add)
                    self.stt(cg[:, 0:NT], ub[:, 0:NT], pc[:, PC_CW0 + ci:PC_CW0 + ci + 1], cg[:, 0:NT], ALU.mult, ALU.add)
                    cs.append(cg)
                self.act(cs[0][:, 0:NT], cs[0][:, 0:NT], AF.Silu)
                self.tt("POOL", actT[:, fb, 0:NT], cs[0][:, 0:NT], cs[1][:, 0:NT], ALU.mult)
        for hf in range(2):
            accs = [self.bank(hold=True) for _ in range(nt)]
            for j in range(6):
                nk = 4 if j < 5 else 2
                wd = self.wnext("dn%d_%d" % (hf, j))
                for t in range(nt):
                    for kk_ in range(nk):
                        fb = j * 4 + kk_
                        self.mm(accs[t][0:TS, 0:512], actT[:, fb, t * TS:(t + 1) * TS], View(wd.b, wd.ap[:, kk_, :]),
                                start=(fb == 0), stop=(fb == NFB - 1))
            for t in range(nt):
                zv = self.z[t][0:TS, hf * 512:(hf + 1) * 512]
                self.tt("DVE", zv, zv, accs[t][0:TS, 0:512], ALU.add)
                accs[t].live = False

    def group(self, sq, g):
        meta = (g == 0)
        TS = 32 if meta else 128
        nt = 1 if meta else self.ntg
        nch = 1 if meta else 4
        NT = TS * nt
        if meta:
            self.memset("POOL", self.z[0][0:32, :], 0.0)
            self.load(self.z[0][16:32, :], self.d_meta)
        else:
            r0 = (g - 1) * self.NTX
            for t in range(nt):
                self.load(self.z[t][:, :], self.d_x[sq, r0 + t * 128:r0 + (t + 1) * 128, :])
        for l in range(self.depth):
            if self.smat_layer != l:
                for bfr, src in zip((self.gkup[0], self.gkb[0], self.w2[0], self.a2[0], self.g2[0]), self.d_smats):
                    self.load(bfr[:, :], src[l])
                self.smat_layer = l
            self.norm_T(l, PC_MIXN, self.hT, TS, nt)
            hT = View(self.hT, self.hT.t[:, :, 0:NT])
            self.gla_branch(l, hT, TS, nt, nch)
            self.hg_branch(l, hT, TS, nt, nch)
            self.rw_branch(l, hT, TS, nt, nch)
            self.merge(l, hT, TS, nt)
            if meta:
                self.memset("POOL", self.z[0][0:16, :], 0.0)
            self.norm_T(l, PC_FFNN, self.hT, TS, nt)
            self.ffn(l, hT, TS, nt)
            if meta:
                self.memset("POOL", self.z[0][0:16, :], 0.0)
        if not meta:
            r0 = (g - 1) * self.NTX
            for t in range(nt):
                rstd = self.rms_rstd(self.z[t], 128)
                ob = self.merged[0]
                self.stt(ob[:, :], self.z[t][:, :], rstd, self.frow[:, :], ALU.mult, ALU.mult)
                tok = self.S.dma("SP", self.lane_of(ob), lambda e, ob=ob, t=t: e.dma_start(
                    out=self.d_y[sq, r0 + t * 128:r0 + (t + 1) * 128, :], in_=ob.t[:, :]), reads=[ob])
                self.final_toks.append(tok)

    def build(self):
        nc = bass.Bass("TRN2", target_bir_lowering=False)
        self.nc = nc
        L = self.depth

        def din(name, shape):
            return nc.dram_tensor(name, list(shape), F32, kind="ExternalInput").ap()
        self.d_x = din("x", [self.nseq, self.SEQ, D])
        self.d_meta = din("meta", [N_META, D])
        self.d_w_in = din("w_in", [L, D, W_IN])
        self.d_w_vres = din("w_in_vres", [max(L - 1, 1), D, 32])
        self.d_wo_gla = din("w_out_gla", [L, 512, D])
        self.d_wo_rw = din("w_out_rw", [L, 512, D])
        self.d_wo_hg = din("w_out_hg", [L, 512, D])
        self.d_w_out = din("w_out", [L, D, D])
        self.d_w_up = din("w_up", [L, D, 2 * D_FF])
        self.d_w_down = din("w_down", [L, D_FF, D])
        d_consts = din("consts", [128, NCC])
        d_pcol = din("pcol", [L, 128, NPC])
        d_gkup = din("gkup", [L, 16, 256])
        d_gkb = din("gkb", [L, 1, 256])
        d_w2 = din("w2", [L, 64, 512])
        d_a2 = din("a2", [L, 64, 512])
        d_g2 = din("g2", [L, 128, 512])
        d_v2 = din("v2", [1, 32, 512])
        d_lg = din("lg", [2, 128, 512])
        d_frow = din("frow", [128, D])
        self.d_y = nc.dram_tensor("y", [self.nseq, self.SEQ, D], F32, kind="ExternalOutput").ap()

        with ExitStack() as st:
            self.st = st
            S = Sched(nc, st)
            self.S = S
            S.add_engine("PE", nc.tensor)
            S.add_engine("ACT", nc.scalar)
            S.add_engine("DVE", nc.vector)
            S.add_engine("POOL", nc.gpsimd)
            S.add_queue("SP", nc.sync)
            self.rings = {}
            self.final_toks = []
            self.pbanks = [Buf(st.enter_context(nc.psum_tensor("pb%d" % i, [128, 512], F32)), "pb%d" % i) for i in range(8)]
            self.pi = -1
            self.consts = self.sb("consts", [128, NCC])
            self.load(self.consts[:, :], d_consts)
            self.pcol = [self.sb("pcol%d" % l, [128, NPC]) for l in range(L)]
            gkup = self.sb("gkup", [16, 256])
            gkb = self.sb("gkb", [1, 256])
            w2 = self.sb("w2", [64, 512])
            a2 = self.sb("a2", [64, 512])
            g2 = self.sb("g2", [128, 512])
            self.gkup, self.gkb, self.w2, self.a2, self.g2 = [gkup] * L, [gkb] * L, [w2] * L, [a2] * L, [g2] * L
            self.d_smats = (d_gkup, d_gkb, d_w2, d_a2, d_g2)
            self.smat_layer = None
            for l in range(L):
                self.load(self.pcol[l][:, :], d_pcol[l])
            self.v2 = self.sb("v2", [32, 512])
            self.load(self.v2[:, :], d_v2[0])
            self.frow = self.sb("frow", [128, D])
            self.load(self.frow[:, :], d_frow)
            self.lb = self.sb("lb", [128, 512])
            self.oml = self.sb("oml", [128, 512])
            self.load(self.lb[:, :], d_lg[1])
            self.load(self.oml[:, :], d_lg[0])
            self.tt("DVE", self.lb[:, :], self.lb[:, :], self.oml[:, :], ALU.subtract)
            self.act(self.lb[:, :], self.lb[:, :], AF.Sigmoid)
            self.ts("DVE", self.oml[:, :], self.lb[:, :], -1.0, 1.0, ALU.mult, ALU.add)
            ntg = self.ntg
            NTM = 128 * ntg
            self.z = [self.sb("z%d" % t, [128, D]) for t in range(ntg)]
            self.merged = [self.sb("mg%d" % t, [128, D]) for t in range(ntg)]
            self.hT = self.sb("hT", [128, 8, NTM], BF16)
            self.actT = self.sb("actT", [128, NFB, NTM], BF16)
            self.yT_gla = _Alias(self.actT, self.actT.t[:, 0:4, :])
            self.yT_hg = _Alias(self.actT, self.actT.t[:, 4:8, :])
            self.yT_rw = _Alias(self.actT, self.actT.t[:, 8:12, :])
            self.mT = _Alias(self.actT, self.actT.t[:, 12:20, :])
            self.vfirst = self.sb("vfirst", [128, 4, NTM])
            self.qTb = self.sb("qTb", [128, 4, 128])
            self.kTb = self.sb("kTb", [128, 4, 128])
            self.qhm = self.sb("qhm", [128, 4, 128])
            self.rb0m = self.sb("rb0m", [128, 4, 128])
            self.prw = self.sb("prw", [128, 16, 129])
            self.Sgla = [self.sb("Sgla%d" % l, [128, 512]) for l in range(L)]
            self.Shg = [self.sb("Shg%d" % l, [128, 512]) for l in range(L)]
            self.Hrw = [self.sb("Hrw%d" % l, [128, 4, 128]) for l in range(L)]
            self.crw = [self.sb("crw%d" % l, [128, 16]) for l in range(L)]
            self.uc = [self.sb("uc%d" % l, [128, 2 * NFB, 2]) for l in range(L)]
            self.NSLOT, self.LOOK = 6, 1
            self.wslots = [self.sb("wslot%d" % i, [128, 4608], BF16) for i in range(self.NSLOT)]
            self.memset("POOL", self.qhm[:, :, :], 0.0)
            self.memset("POOL", self.rb0m[:, :, :], 0.0)
            self.memset("POOL", self.prw[:, :, :], 0.0)
            for nm in ("rw_wta", "rw_wtb"):
                bfr = self.rot(nm, [128, 128], F32, n=1)
                self.memset("POOL", bfr[:, :], 0.0)
            self.wseq = []
            for sq in range(self.nseq):
                for g in range(1 + self.nxg):
                    for l in range(L):
                        self.wseq += self.wsched_layer(l)
            self.wpos, self.wissued = 0, 0
            for sq in range(self.nseq):
                for l in range(L):
                    for bfr in (self.Sgla[l], self.Shg[l]):
                        self.memset("POOL", bfr[:, :], 0.0)
                    self.memset("POOL", self.Hrw[l][:, :, :], 0.0)
                    self.memset("POOL", self.crw[l][:, :], 0.0)
                    self.memset("POOL", self.uc[l][:, :, :], 0.0)
                for g in range(1 + self.nxg):
                    self.group(sq, g)
            assert self.wpos == len(self.wseq)
            for tok in self.final_toks:
                S.wait_tok("SP", tok)
            print("program: %d instructions, %d waits, %d sems, sbuf left %d" %
                  (S.n_ins, S.n_wait, S.nsem, nc.sbuf_bytes_remaining))
        return nc


def make_consts():
    c = np.zeros((128, NCC), np.float32)
    i = np.arange(128)
    same = (i[:, None] // 32) == (i[None, :] // 32)
    J, I = i[:, None], i[None, :]
    ref = 32 * (I // 32) + 15
    c[:, CI:CI + 128] = np.eye(128)
    c[:, CM1:CM1 + 128] = same * ((J <= I).astype(np.float32) - (J <= ref).astype(np.float32))
    c[:, CM3:CM3 + 128] = same * (J <= I)
    c[:, CMSU:CMSU + 128] = same * (J < I)
    c[:, CMSL:CMSL + 128] = same * (J > I)
    c[:, CHO:CHO + 128] = ((J // 64) == (I // 64))
    for k in range(4):
        c[:, CSEL_ + k] = (i // 32 == k)
    for k in range(2):
        c[:, CHS + k] = (i // 64 == k)
    c[:, CONE:CONE + 128] = 1.0
    c[:, CRST:CRST + 128] = (I % 32 != 0) * np.ones((128, 1))
    return c


def _cols(v, nrows=128):
    v = np.asarray(v, np.float32).reshape(-1)
    nb = (len(v) + 127) // 128
    out = np.zeros((128, nb), np.float32)
    for b in range(nb):
        seg = v[b * 128:(b + 1) * 128]
        out[:len(seg), b] = seg
    return out


def make_pcol(inp, l):
    p = np.zeros((128, NPC), np.float32)
    mu = np.asarray(inp["rw_mu"][l], np.float32)
    p[:, 0:4] = _cols(mu[0:512])
    p[:, 4:5] = _cols(mu[512:576])
    p[:, 5:9] = _cols(mu[576:1088])
    p[:, 9:13] = _cols(mu[1088:1600])
    p[:, 13:14] = _cols(mu[1600:1664])
    p[:, 14:15] = _cols(mu[1664:1792])
    if l >= 1:
        p[:, 15:16] = _cols(inp["rw_mu_vres"][l - 1])
        p[:, PC_V0:PC_V0 + 4] = _cols(inp["rw_v0"][l - 1])
    p[:, PC_W0:PC_W0 + 4] = _cols(inp["rw_w0"][l])
    p[:, PC_A0:PC_A0 + 4] = _cols(inp["rw_a0"][l])
    p[:, PC_KK:PC_KK + 4] = _cols(inp["rw_kk"][l])
    p[:, PC_KA:PC_KA + 4] = _cols(inp["rw_ka"][l])
    p[:, PC_RK:PC_RK + 4] = _cols(np.asarray(inp["rw_rk"][l]).reshape(-1))
    p[:, PC_MIXN:PC_MIXN + 8] = _cols(inp["mix_norm"][l])
    p[:, PC_FFNN:PC_FFNN + 8] = _cols(inp["ffn_norm"][l])
    p[:, PC_GLAN:PC_GLAN + 1] = _cols(inp["gla_norm"][l])
    p[:, PC_HGN:PC_HGN + 1] = _cols(inp["hg_norm"][l])
    p[:, PC_LNW:PC_LNW + 4] = _cols(inp["rw_ln_w"][l])
    p[:, PC_LNB:PC_LNB + 4] = _cols(inp["rw_ln_b"][l])
    cw = np.asarray(inp["conv_w"][l], np.float32)
    p[:, PC_CW0:PC_CW0 + 44] = _cols(cw[0])
    p[:, PC_CW1:PC_CW1 + 44] = _cols(cw[1])
    p[:, PC_CW2:PC_CW2 + 44] = _cols(cw[2])
    p[:, PC_CB:PC_CB + 44] = _cols(inp["conv_b"][l])
    return p


_PROG_CACHE = {}


def run(inputs, nseq, nxg, depth, ncores, taps=()):
    key = (nseq, nxg, depth, tuple(taps))
    if key not in _PROG_CACHE:
        bld = Builder(nseq, nxg, depth, taps=taps)
        _PROG_CACHE[key] = (bld.build(), bld)
    nc, bld = _PROG_CACHE[key]
    f = lambda a: np.ascontiguousarray(np.asarray(a, np.float32))
    L = depth
    shared = {
        "meta": f(inputs["meta"]),
        "w_in": f(inputs["w_in"][:L]),
        "w_in_vres": f(inputs["w_in_vres"][:max(L - 1, 1)]),
        "w_out_gla": f(inputs["w_out_gla"][:L]),
        "w_out_rw": f(inputs["w_out_rw"][:L]),
        "w_out_hg": f(inputs["w_out_hg"][:L]),
        "w_out": f(inputs["w_out"][:L]),
        "w_up": f(inputs["w_up"][:L]),
        "w_down": f(inputs["w_down"][:L]),
        "consts": make_consts(),
        "pcol": np.stack([make_pcol(inputs, l) for l in range(L)]),
        "gkup": f(inputs["gla_gk_up"][:L]),
        "gkb": f(np.asarray(inputs["gla_gk_bias"])[:L, None, :]),
        "w2": f(inputs["rw_w2"][:L]),
        "a2": f(inputs["rw_a2"][:L]),
        "g2": f(inputs["rw_g2"][:L]),
        "v2": f(inputs["rw_v2"][:1]),
        "lg": f(np.broadcast_to(np.asarray(inputs["hg_lb_logits"], np.float32)[:2, None, :], (2, 128, 512))),
        "frow": f(np.broadcast_to(np.asarray(inputs["final_norm"], np.float32)[None, :], (128, D))),
    }
    x = f(inputs["x"])
    in_maps = []
    for c in range(ncores):
        m = dict(shared)
        m["x"] = np.ascontiguousarray(x[c * nseq:(c + 1) * nseq, :nxg * bld.NTX])
        in_maps.append(m)
    res = run_bass_kernel_spmd(nc, in_maps, core_ids=list(range(ncores)))
    return res, bld


def kernel(**inputs):
    res, bld = run(inputs, 2, 8, DEPTH, 8)
    return np.concatenate([np.asarray(r["y"]) for r in res.results], axis=0).astype(np.float32)
```

```python
import numpy as np
from contextlib import ExitStack
import concourse.bass as bass
import concourse.mybir as mybir
from concourse.bass_utils import run_bass_kernel_spmd

F32 = mybir.dt.float32
BF16 = mybir.dt.bfloat16
F32R = mybir.dt.float32r
FP32R = True
AF = mybir.ActivationFunctionType
ALU = mybir.AluOpType

D = 1024
DEPTH = 2
N_META = 16
W_IN = 8464
D_FF = 2816
NFB = 22
SEM_LIMIT = 30000
ATTACH_WAIT = True
RNAMES = {"qts", "kts", "pt", "A1", "A7", "B0", "sn0", "sn1", "sn2", "rw_ea", "rw_er", "rw_einv", "rw_btil",
          "rw_aak", "rw_nT", "rw_nN", "rw_arb0", "rw_arb1", "rw_x", "rw_wta", "rw_wtb", "rw_ucm"}
EPS = 1e-6
RW_GN_EPS = 64e-5

C_GLA_QK, C_GLA_V, C_GLA_GKG = 0, 512, 1024
C_HG_Q, C_HG_F, C_HG_I, C_HG_G = 1552, 2064, 2576, 3088
C_GATE_GLA, C_GATE_RW, C_GATE_HG = 3600, 4624, 5648
C_RW = 6672

CI, CM1, CM3, CMSU, CMSL, CHO, CSEL_, CHS, CONE, CRST = 0, 128, 256, 384, 512, 640, 768, 772, 774, 902
NCC = 1030
PC_MU, PC_W0, PC_A0, PC_V0, PC_KK, PC_KA, PC_RK, PC_MIXN, PC_FFNN, PC_GLAN, PC_HGN, PC_LNW, PC_LNB = \
    0, 16, 20, 24, 28, 32, 36, 40, 48, 56, 57, 58, 62
PC_CW0, PC_CW1, PC_CW2, PC_CB = 66, 110, 154, 198
NPC = 242


class View:
    def __init__(self, b, ap):
        self.b = b
        self.ap = ap


class Buf:
    def __init__(self, t, name=""):
        self.t = t
        self.name = name
        self.w = None
        self.r = {}
        self.lane = None
        self.live = False
        self.rnd = False

    def __getitem__(self, k):
        return View(self, self.t[k])


class _Alias:
    def __init__(self, parent, ap=None, pattern=None, a=None):
        self.b = parent
        if ap is None:
            ap = parent.t[:, :].rearrange(pattern, a=a)
        self.t = ap

    def __getitem__(self, k):
        return View(self.b, self.t[k])


class Lane:
    def __init__(self, S, name, step):
        self.S, self.name, self.step = S, name, step
        self.epoch, self.cnt = 0, 0
        self.sems = [S.new_sem(name + "_0")]

    def next_token(self):
        if self.cnt + self.step > SEM_LIMIT:
            self.epoch += 1
            self.cnt = 0
            self.sems.append(self.S.new_sem("%s_%d" % (self.name, self.epoch)))
        self.cnt += self.step
        return (self.name, self.epoch, self.cnt)


class Sched:
    def __init__(self, nc, stack):
        self.nc, self.stack = nc, stack
        self.lanes, self.h, self.seen = {}, {}, {}
        self.n_ins = 0
        self.n_wait = 0
        self.nsem = 0

    def new_sem(self, name):
        self.nsem += 1
        return self.stack.enter_context(self.nc.semaphore(name))

    def add_engine(self, name, handle):
        self.lanes[name] = Lane(self, name, 1)
        self.h[name] = handle
        self.seen[name] = {}

    def add_queue(self, name, handle):
        self.h[name] = handle
        self.seen[name] = {}

    def dma_lane(self, name):
        ln = Lane(self, name, 16)
        self.lanes[name] = ln
        return ln

    def _deps(self, eng, reads, writes, attach=ATTACH_WAIT):
        deps = []
        for b in reads:
            if b.w is not None:
                deps.append(b.w)
        for b in writes:
            if b.w is not None and b.w[0] != eng:
                deps.append(b.w)
            for en, tok in b.r.items():
                if en != eng:
                    deps.append(tok)
        if eng == "PE":
            deps = [d for d in deps if d[0] != "PE"]
        best = {}
        for d in deps:
            k = (d[0], d[1])
            if k not in best or best[k] < d[2]:
                best[k] = d[2]
        seen = self.seen[eng]
        h = self.h[eng]
        need = []
        for (ln, ep), c in best.items():
            if seen.get((ln, ep), 0) >= c:
                continue
            need.append((self.lanes[ln].sems[ep], c))
            seen[(ln, ep)] = c
        last = need.pop() if (need and attach) else None
        for sem, c in need:
            h.wait_ge(sem, c)
            self.n_wait += 1
        return last

    def op(self, eng, fn, reads=(), writes=(), attach=True):
        last = self._deps(eng, reads, writes, attach=(ATTACH_WAIT and attach and eng != "PE"))
        ins = fn(self.h[eng])
        if last is not None:
            ins._wait_ge(last[0], last[1])
        lane = self.lanes[eng]
        tok = lane.next_token()
        ins.then_inc(lane.sems[tok[1]], 1)
        self.n_ins += 1
        for b in reads:
            b.r[eng] = tok
        for b in writes:
            b.w = tok
            b.r = {}
        return tok

    def dma(self, queue, lane, fn, reads=(), writes=()):
        self._deps(queue, reads, writes, attach=False)
        ins = fn(self.h[queue])
        tok = lane.next_token()
        ins.then_inc(lane.sems[tok[1]], 16)
        self.n_ins += 1
        for b in reads:
            b.r[lane.name] = tok
        for b in writes:
            b.w = tok
            b.r = {}
        return tok

    def wait_tok(self, eng, tok):
        self.h[eng].wait_ge(self.lanes[tok[0]].sems[tok[1]], tok[2])


class Ring:
    def __init__(self, bufs):
        self.bufs = bufs
        self.i = -1


class Builder:
    def __init__(self, nseq, nxg, depth=DEPTH, ntg=2, taps=()):
        self.nseq, self.nxg, self.depth, self.ntg = nseq, nxg, depth, ntg
        self.taps = set(taps)
        self.tap_names = []
        self.NTX = 128 * ntg
        self.SEQ = nxg * self.NTX

    def sb(self, name, shape, dt=F32, r=False):
        t = self.st.enter_context(self.nc.sbuf_tensor("s_" + name, list(shape), dt))
        b = Buf(t, name)
        b.rnd = bool(r) and FP32R and dt == F32
        return b

    def rot(self, name, shape=None, dt=F32, n=2, r=False):
        r = r or (name in RNAMES)
        if name not in self.rings:
            self.rings[name] = Ring([self.sb("%s_%d" % (name, i), shape, dt, r=r) for i in range(n)])
        r = self.rings[name]
        r.i = (r.i + 1) % len(r.bufs)
        return r.bufs[r.i]

    def bank(self, hold=False):
        for _ in range(8):
            self.pi = (self.pi + 1) % 8
            b = self.pbanks[self.pi]
            if not b.live:
                b.live = hold
                return b
        raise RuntimeError("no free psum bank")

    def lane_of(self, buf):
        if buf.lane is None:
            buf.lane = self.S.dma_lane("dl_" + buf.name)
        return buf.lane

    @staticmethod
    def _bufs(*vs):
        out = []
        for v in vs:
            if isinstance(v, View) and v.b not in out:
                out.append(v.b)
        return out

    @staticmethod
    def _a(v):
        return v.ap if isinstance(v, View) else v

    @staticmethod
    def _o(v):
        if v.b.rnd and v.ap.dtype == F32:
            return v.ap.bitcast(F32R)
        return v.ap

    def mm(self, out, lhsT, rhs, start=True, stop=True, row=None):
        rd = self._bufs(lhsT, rhs) + ([] if start else [out.b])
        kw = {} if row is None else {"tile_position": (row, 0)}
        la, ra = lhsT.ap, rhs.ap
        if lhsT.b.rnd and rhs.b.rnd and la.dtype == F32 and ra.dtype == F32 and ra.shape[-1] % 2 == 0:
            la = la.bitcast(F32R)
            ra = ra.bitcast(F32R)
        self.S.op("PE", lambda e: e.matmul(out.ap, la, ra, start=start, stop=stop, **kw),
                  reads=rd, writes=[out.b])

    def tr(self, out, in_):
        n = in_.ap.shape[0]
        ident = self.consts[0:n, CI:CI + n]
        self.S.op("PE", lambda e: e.transpose(out.ap, in_.ap, ident.ap),
                  reads=self._bufs(in_, ident), writes=[out.b])

    def act(self, out, in_, func, bias=0.0, scale=1.0, accum=None, eng="ACT"):
        kw = {}
        if accum is not None:
            kw["accum_out"] = accum.ap
        wr = self._bufs(out) + (self._bufs(accum) if accum is not None else [])
        self.S.op(eng, lambda e: e.activation(out=self._o(out), in_=in_.ap, func=func, bias=self._a(bias),
                                              scale=self._a(scale), **kw),
                  reads=self._bufs(in_, bias, scale), writes=wr, attach=(accum is None))

    def ts(self, eng, out, in0, s1, s2, op0, op1=None):
        if op1 is None:
            self.S.op(eng, lambda e: e.tensor_scalar(self._o(out), in0.ap, self._a(s1), None, op0),
                      reads=self._bufs(in0, s1), writes=[out.b])
        else:
            self.S.op(eng, lambda e: e.tensor_scalar(self._o(out), in0.ap, self._a(s1), self._a(s2), op0, op1),
                      reads=self._bufs(in0, s1, s2), writes=[out.b])

    def stt(self, out, in0, scalar, in1, op0, op1, eng="DVE"):
        self.S.op(eng, lambda e: e.scalar_tensor_tensor(self._o(out), in0.ap, self._a(scalar), in1.ap, op0, op1),
                  reads=self._bufs(in0, scalar, in1), writes=[out.b])

    def tt(self, eng, out, in0, in1, op):
        self.S.op(eng, lambda e: e.tensor_tensor(self._o(out), in0.ap, in1.ap, op),
                  reads=self._bufs(in0, in1), writes=[out.b])

    def cp(self, eng, out, in_):
        if eng == "ACT":
            self.S.op(eng, lambda e: e.copy(self._o(out), in_.ap), reads=[in_.b], writes=[out.b])
        else:
            self.S.op(eng, lambda e: e.tensor_copy(self._o(out), in_.ap), reads=[in_.b], writes=[out.b])

    def memset(self, eng, out, val):
        if out.b.rnd:
            shp = list(out.ap.shape)
            n = int(np.prod(shp[1:]))
            src = self.consts.t[0:shp[0], 0:n]
            if len(shp) == 3:
                src = src.rearrange("p (a b) -> p a b", a=shp[1])
            self.ts(eng, out, View(self.consts, src), 0.0, float(val), ALU.mult, ALU.add)
            return
        self.S.op(eng, lambda e: e.memset(out.ap, val), reads=[], writes=[out.b])

    def rsqrt(self, out, in_):
        self.act(out, in_, AF.Ln)
        self.act(out, out, AF.Exp, scale=-0.5)

    def recip(self, out, in_):
        self.S.op("DVE", lambda e: e.reciprocal(out.ap, in_.ap), reads=[in_.b], writes=[out.b])

    def load(self, out, dram_ap, queue="SP"):
        return self.S.dma(queue, self.lane_of(out.b), lambda e: e.dma_start(out=out.ap, in_=dram_ap),
                          writes=[out.b])

    def tap(self, name, view, shape):
        if name not in self.taps:
            return
        nm = "tap_" + name
        if nm in self.tap_names:
            return
        self.tap_names.append(nm)
        d = self.nc.dram_tensor(nm, list(shape), view.ap.dtype, kind="ExternalOutput").ap()
        tb = Buf(None, nm)
        tok = self.S.dma("SP", self.lane_of(tb), lambda e: e.dma_start(out=d, in_=view.ap), reads=[view.b])
        self.final_toks.append(tok)

    def wsched_layer(self, l):
        seq = []
        win = self.d_w_in[l]

        def L(key, src, nkb, c0, n):
            seq.append((key, src, nkb, c0, n))
        L("gla_qk", win, 8, C_GLA_QK, 512)
        L("gla_v", win, 8, C_GLA_V, 512)
        L("gla_gkg", win, 8, C_GLA_GKG, 528)
        L("hg_q", win, 8, C_HG_Q, 512)
        L("hg_f", win, 8, C_HG_F, 512)
        L("hg_i", win, 8, C_HG_I, 512)
        L("hg_g", win, 8, C_HG_G, 512)
        L("rw_r", win, 8, C_RW, 512)
        L("rw_wk", win, 8, C_RW + 512, 576)
        L("rw_va", win, 8, C_RW + 1088, 576)
        L("rw_g", win, 8, C_RW + 1664, 128)
        if l >= 1:
            L("rw_vr", self.d_w_vres[l - 1], 8, 0, 32)
        for nm, c0, wo in (("gla", C_GATE_GLA, self.d_wo_gla), ("rw", C_GATE_RW, self.d_wo_rw),
                           ("hg", C_GATE_HG, self.d_wo_hg)):
            L("gate0_" + nm, win, 8, c0, 512)
            L("gate1_" + nm, win, 8, c0 + 512, 512)
            L("wo_" + nm, wo[l], 4, 0, 1024)
        L("wout0", self.d_w_out[l], 8, 0, 512)
        L("wout1", self.d_w_out[l], 8, 512, 512)
        for j in range(6):
            n = 512 if j < 5 else 256
            L("upg%d" % j, self.d_w_up[l], 8, j * 512, n)
            L("upv%d" % j, self.d_w_up[l], 8, D_FF + j * 512, n)
        for hf in range(2):
            for j in range(6):
                nk = 4 if j < 5 else 2
                L("dn%d_%d" % (hf, j), self.d_w_down[l][j * 512:j * 512 + nk * 128, :], nk, hf * 512, 512)
        return seq

    def wnext(self, key):
        i = self.wpos
        assert self.wseq[i][0] == key, (self.wseq[i][0], key)
        while self.wissued < min(len(self.wseq), i + self.LOOK + 1):
            j = self.wissued
            _, src, nkb, c0, n = self.wseq[j]
            slot = self.wslots[j % self.NSLOT]
            dst = View(slot, slot.t[:, 0:nkb * n].rearrange("p (k n) -> p k n", k=nkb))
            srcap = src.rearrange("(k p) n -> p k n", p=128)[:, :, c0:c0 + n]
            self.S.dma("POOL", self.lane_of(slot), lambda e, d=dst, s=srcap: e.dma_start(out=d.ap, in_=s),
                       writes=[slot])
            self.wissued += 1
        self.wpos += 1
        _, src, nkb, c0, n = self.wseq[i]
        slot = self.wslots[i % self.NSLOT]
        return View(slot, slot.t[:, 0:nkb * n].rearrange("p (k n) -> p k n", k=nkb))

    def proj_fm(self, wv, c0, m, hT, ntok, out):
        for kc in range(8):
            self.mm(out, View(wv.b, wv.ap[:, kc, c0:c0 + m]), View(hT.b, hT.ap[:, kc, 0:ntok]),
                    start=(kc == 0), stop=(kc == 7))

    def proj_tm(self, wv, c0, n, hT, t0, ts_, out):
        for kc in range(8):
            self.mm(out, View(hT.b, hT.ap[:, kc, t0:t0 + ts_]), View(wv.b, wv.ap[:, kc, c0:c0 + n]),
                    start=(kc == 0), stop=(kc == 7))

    def rms_rstd(self, zt, TS):
        sq = self.merged[-1]
        st = self.rot("rms_st", [128, 4], F32, n=3)
        self.act(sq[0:TS, :], zt[0:TS, :], AF.Square, accum=st[0:TS, 0:1])
        self.ts("DVE", st[0:TS, 1:2], st[0:TS, 0:1], 1.0 / D, EPS, ALU.mult, ALU.add)
        self.rsqrt(st[0:TS, 3:4], st[0:TS, 1:2])
        return st[0:TS, 3:4]

    def norm_T(self, l, pc_norm, hT, TS, nt):
        for t in range(nt):
            zt = self.z[t]
            rstd = self.rms_rstd(zt, TS)
            hn = self.merged[-1]
            self.act(hn[0:TS, :], zt[0:TS, :], AF.Identity, scale=rstd)
            for half in range(2):
                ps = self.bank()
                for q in range(4):
                    kc = half * 4 + q
                    self.tr(ps[0:128, q * TS:(q + 1) * TS], hn[0:TS, kc * 128:(kc + 1) * 128])
                for q in range(4):
                    kc = half * 4 + q
                    eng = "ACT" if q % 2 == 0 else "DVE"
                    dst = hT[:, kc, t * TS:(t + 1) * TS]
                    col = self.pcol[l][:, pc_norm + kc:pc_norm + kc + 1]
                    if eng == "ACT":
                        self.act(dst, ps[0:128, q * TS:(q + 1) * TS], AF.Identity, scale=col)
                    else:
                        self.ts("DVE", dst, ps[0:128, q * TS:(q + 1) * TS], col, None, ALU.mult)

    def chunk_gla_tile(self, tag, dk, qT, kT, ktm, vtm, glog, gs, qscale, St, gate_sg, yT, normcol, t, TS, nch):
        F = 4 * dk
        nb = F // 128
        hpb = 128 // dk
        W = 128 * hpb
        C = self.consts
        qts = self.rot("qts", [128, 4, 128], F32, n=1)
        kts = self.rot("kts", [128, 4, 128], F32, n=1)
        qhm = self.qhm
        dec = self.rot("dec", [128, 4, 4], F32, n=1)
        e3a = self.rot("e3a", [128, 4, 128], F32, n=1)
        for b in range(nb):
            ps = self.bank()
            gl = glog[0:TS, b * 128:(b + 1) * 128]
            self.mm(ps[0:128, 0:TS], gl, C[0:TS, CM1:CM1 + TS])
            self.mm(ps[0:128, 128:128 + TS], gl, C[0:TS, CM3:CM3 + TS])
            self.mm(ps[0:128, 256:256 + nch], gl, C[0:TS, CSEL_:CSEL_ + nch])
            e1 = self.rot("e1", [128, 128], F32, n=1)
            e2 = self.rot("e2", [128, 128], F32, n=1)
            self.act(e1[:, 0:TS], ps[0:128, 0:TS], AF.Exp, scale=gs)
            self.act(e2[:, 0:TS], ps[0:128, 0:TS], AF.Exp, scale=-gs)
            self.act(e3a[:, b, 0:TS], ps[0:128, 128:128 + TS], AF.Exp, scale=gs)
            self.act(dec[:, b, 0:nch], ps[0:128, 256:256 + nch], AF.Exp, scale=gs)
            qv = View(qT.b, qT.ap[:, b, :])
            kv = View(kT.b, kT.ap[:, b, :])
            self.stt(qts[:, b, 0:TS], qv, qscale, e1[:, 0:TS], ALU.mult, ALU.mult)
            self.tt("DVE", kts[:, b, 0:TS], kv, e2[:, 0:TS], ALU.mult)
        ps4 = self.bank()
        self.mm(ps4[0:TS, 0:F], C[0:TS, CMSL:CMSL + TS], glog[0:TS, 0:F])
        e4 = self.rot("A6", [128, 512], F32, n=1)
        self.act(e4[0:TS, 0:F], ps4[0:TS, 0:F], AF.Exp, scale=gs)
        khat = self.rot("A7", [128, 512], F32, n=1)
        self.tt("DVE", khat[0:TS, 0:F], ktm[0:TS, 0:F], e4[0:TS, 0:F], ALU.mult)
        snaps = [St] + [self.rot("sn%d" % i, [128, 512], F32, n=1) for i in range(nch - 1)]

        def upd(c):
            for b in range(nb):
                psi = self.bank()
                self.mm(psi[0:128, 0:W], khat[32 * c:32 * c + 32, b * 128:(b + 1) * 128], vtm[32 * c:32 * c + 32, b * W:(b + 1) * W],
                        row=32 * c)
                dst = snaps[c + 1] if c + 1 < nch else St
                self.stt(dst[:, b * W:(b + 1) * W], snaps[c][:, b * W:(b + 1) * W], dec[:, b, c:c + 1], psi[0:128, 0:W], ALU.mult, ALU.add)
        for c in range(nch - 1):
            upd(c)
        pso = self.bank(hold=True)
        for h in range(4):
            b = (h * dk) // 128
            po = (h * dk) % 128
            hb = h % hpb
            if po == 0:
                for c in range(nch):
                    self.stt(qhm[:, c, c * 32:(c + 1) * 32], View(qT.b, qT.ap[:, b, c * 32:(c + 1) * 32]), qscale,
                             e3a[:, b, c * 32:(c + 1) * 32], ALU.mult, ALU.mult)
            pss = self.bank()
            self.mm(pss[0:TS, 0:TS], kts[po:po + dk, b, 0:TS], qts[po:po + dk, b, 0:TS])
            pt = self.rot("pt", [128, 128], F32, n=2)
            self.tt("DVE", pt[0:TS, 0:TS], pss[0:TS, 0:TS], C[0:TS, CM3:CM3 + TS], ALU.mult)
            oh = pso[0:TS, h * 128:(h + 1) * 128]
            self.mm(oh, pt[0:TS, 0:TS], vtm[0:TS, h * 128:(h + 1) * 128], start=True, stop=False)
            for c in range(nch):
                self.mm(oh, qhm[po:po + dk, c, 0:TS], snaps[c][po:po + dk, b * W + hb * 128:b * W + (hb + 1) * 128],
                        start=False, stop=(c == nch - 1))
        upd(nch - 1)
        st = self.rot("hn_st", [128, 16], F32, n=1)
        junk = self.rot("hn_junk", [128, 128], F32, n=1)
        for h in range(4):
            self.act(junk[0:TS, :], pso[0:TS, h * 128:(h + 1) * 128], AF.Square, accum=st[0:TS, h:h + 1])
        self.ts("DVE", st[0:TS, 4:8], st[0:TS, 0:4], 1.0 / 128, EPS, ALU.mult, ALU.add)
        self.rsqrt(st[0:TS, 12:16], st[0:TS, 4:8])
        ytm = self.rot("A8", [128, 512], F32, n=1)
        for h in range(4):
            self.stt(ytm[0:TS, h * 128:(h + 1) * 128], pso[0:TS, h * 128:(h + 1) * 128], st[0:TS, 12 + h:13 + h],
                     gate_sg[0:TS, h * 128:(h + 1) * 128], ALU.mult, ALU.mult)
        pso.live = False
        pst = self.bank()
        for h in range(4):
            self.tr(pst[0:128, h * TS:(h + 1) * TS], ytm[0:TS, h * 128:(h + 1) * 128])
        for h in range(4):
            self.act(yT[:, h, t * TS:(t + 1) * TS], pst[0:128, h * TS:(h + 1) * TS], AF.Identity, scale=normcol)

    def gla_branch(self, l, hT, TS, nt, nch):
        NT = TS * nt
        C = self.consts
        w_qk = self.wnext("gla_qk")
        w_v = self.wnext("gla_v")
        w_g = self.wnext("gla_gkg")
        qT = self.qTb
        kT = self.kTb
        for t in range(nt):
            t0 = t * TS
            hTt = View(hT.b, hT.ap[:, :, t0:t0 + TS])
            for b in range(2):
                ps = self.bank()
                self.proj_fm(w_qk, b * 128, 128, hTt, TS, ps[0:128, 0:TS])
                self.cp("ACT", qT[:, b, 0:TS], ps[0:128, 0:TS])
                ps = self.bank()
                self.proj_fm(w_qk, 256 + b * 128, 128, hTt, TS, ps[0:128, 0:TS])
                self.cp("DVE", kT[:, b, 0:TS], ps[0:128, 0:TS])
            gkT = self.rot("A5", [128, 512], F32, n=1)
            ps = self.bank()
            self.proj_fm(w_g, 0, 16, hTt, TS, ps[0:16, 0:TS])
            self.cp("ACT", gkT[0:16, 0:TS], ps[0:16, 0:TS])
            ps = self.bank()
            for b in range(2):
                self.tr(ps[0:TS, b * 128:(b + 1) * 128], kT[:, b, 0:TS])
            ktm = self.rot("A0", [128, 512], F32, n=1)
            self.cp("ACT", ktm[0:TS, 0:256], ps[0:TS, 0:256])
            ps = self.bank()
            self.proj_tm(w_v, 0, 512, hT, t0, TS, ps[0:TS, 0:512])
            vtm = self.rot("A1", [128, 512], F32, n=1)
            self.cp("DVE", vtm[0:TS, :], ps[0:TS, 0:512])
            ps = self.bank()
            self.proj_tm(w_g, 16, 512, hT, t0, TS, ps[0:TS, 0:512])
            sg = self.rot("A2", [128, 512], F32, n=1)
            self.act(sg[0:TS, :], ps[0:TS, 0:512], AF.Silu)
            ps = self.bank()
            self.mm(ps[0:TS, 0:256], gkT[0:16, 0:TS], self.gkup[l][0:16, :], start=True, stop=False)
            self.mm(ps[0:TS, 0:256], C[0:1, CONE:CONE + TS], self.gkb[l][0:1, :], start=False, stop=True)
            glog = self.rot("A3", [128, 512], F32, n=1)
            self.act(glog[0:TS, 0:256], ps[0:TS, 0:256], AF.Exp, scale=-1.0)
            self.act(glog[0:TS, 0:256], glog[0:TS, 0:256], AF.Ln, bias=1.0)
            self.chunk_gla_tile("gla", 64, View(qT, qT.t[:, :, 0:TS]), View(kT, kT.t[:, :, 0:TS]),
                                ktm, vtm, glog, -1.0 / 16.0, 0.125, self.Sgla[l], sg, self.yT_gla,
                                self.pcol[l][:, PC_GLAN:PC_GLAN + 1], t, TS, nch)

    def hg_branch(self, l, hT, TS, nt, nch):
        NT = TS * nt
        w_q = self.wnext("hg_q")
        w_f = self.wnext("hg_f")
        w_i = self.wnext("hg_i")
        w_g = self.wnext("hg_g")
        qT = self.qTb
        kT = self.kTb
        for t in range(nt):
            t0 = t * TS
            hTt = View(hT.b, hT.ap[:, :, t0:t0 + TS])
            for b in range(4):
                ps = self.bank()
                self.proj_fm(w_q, b * 128, 128, hTt, TS, ps[0:128, 0:TS])
                self.cp("ACT" if b % 2 == 0 else "DVE", qT[:, b, 0:TS], ps[0:128, 0:TS])
            ps = self.bank()
            self.proj_tm(w_f, 0, 512, hT, t0, TS, ps[0:TS, 0:512])
            sgf = self.rot("A4", [128, 512], F32, n=1)
            self.act(sgf[0:TS, :], ps[0:TS, 0:512], AF.Sigmoid)
            ktm = self.rot("A0", [128, 512], F32, n=1)
            glog = self.rot("A3", [128, 512], F32, n=1)
            if l == 0:
                self.ts("DVE", ktm[0:TS, :], sgf[0:TS, :], -1.0, 1.0, ALU.mult, ALU.add)
                self.ts("POOL", sgf[0:TS, :], sgf[0:TS, :], 1e-30, None, ALU.max)
                self.act(glog[0:TS, :], sgf[0:TS, :], AF.Ln)
            else:
                tmp = self.rot("A5", [128, 512], F32, n=1)
                self.tt("DVE", tmp[0:TS, :], sgf[0:TS, :], self.oml[0:TS, :], ALU.mult)
                self.tt("DVE", ktm[0:TS, :], self.oml[0:TS, :], tmp[0:TS, :], ALU.subtract)
                self.stt(tmp[0:TS, :], tmp[0:TS, :], 1e-30, self.lb[0:TS, :], ALU.max, ALU.add)
                self.act(glog[0:TS, :], tmp[0:TS, :], AF.Ln)
            ps = self.bank()
            for b in range(4):
                self.tr(ps[0:128, b * TS:(b + 1) * TS], ktm[0:TS, b * 128:(b + 1) * 128])
            for b in range(4):
                self.cp("ACT" if b % 2 == 0 else "DVE", kT[:, b, 0:TS], ps[0:128, b * TS:(b + 1) * TS])
            ps = self.bank()
            self.proj_tm(w_i, 0, 512, hT, t0, TS, ps[0:TS, 0:512])
            vtm = self.rot("A1", [128, 512], F32, n=1)
            self.cp("DVE", vtm[0:TS, :], ps[0:TS, 0:512])
            ps = self.bank()
            self.proj_tm(w_g, 0, 512, hT, t0, TS, ps[0:TS, 0:512])
            sg = self.rot("A2", [128, 512], F32, n=1)
            self.act(sg[0:TS, :], ps[0:TS, 0:512], AF.Silu)
            self.chunk_gla_tile("hg", 128, View(qT, qT.t[:, :, 0:TS]), View(kT, kT.t[:, :, 0:TS]),
                                ktm, vtm, glog, 1.0, 1.0, self.Shg[l], sg, self.yT_hg,
                                self.pcol[l][:, PC_HGN:PC_HGN + 1], t, TS, nch)

    def rw_branch(self, l, hT, TS, nt, nch):
        C = self.consts
        pc = self.pcol[l]
        w_r = self.wnext("rw_r")
        w_wk = self.wnext("rw_wk")
        w_va = self.wnext("rw_va")
        w_g = self.wnext("rw_g")
        w_vr = self.wnext("rw_vr") if l >= 1 else None
        pw = self.prw
        H = self.Hrw[l]
        blocks = []
        for b in range(4):
            blocks.append((w_r, b * 128, 128))
        blocks.append((w_wk, 0, 64))
        for b in range(4):
            blocks.append((w_wk, 64 + b * 128, 128))
        for b in range(4):
            blocks.append((w_va, b * 128, 128))
        blocks.append((w_va, 512, 64))
        blocks.append((w_g, 0, 128))
        if l >= 1:
            blocks.append((w_vr, 0, 32))
        nblk = len(blocks)
        for t in range(nt):
            t0 = t * TS
            self.cp("POOL", View(pw, pw.t[:, :, 0]), self.crw[l][:, 0:16])
            for j, (wv, c0, m) in enumerate(blocks):
                ps = self.bank()
                self.proj_fm(wv, c0, m, View(hT.b, hT.ap[:, :, t0:t0 + TS]), TS, ps[0:m, 0:TS])
                self.cp("ACT" if j % 2 == 0 else "DVE", pw[0:m, j, 1:1 + TS], ps[0:m, 0:TS])
            self.cp("POOL", self.crw[l][:, 0:16], View(pw, pw.t[:, :, TS]))
            s_b = self.rot("B0", [128, 2048], F32, n=1)
            s = _Alias(s_b, s_b.t[:, :].rearrange("p (j t) -> p j t", j=16))
            self.tt("DVE", s[:, :, 0:TS], pw[:, :, 0:TS], pw[:, :, 1:1 + TS], ALU.subtract)
            mub = View(pc, pc.t[:, PC_MU:PC_MU + 16].unsqueeze(2).to_broadcast([128, 16, TS]))
            self.tt("DVE", s[:, :, 0:TS], s[:, :, 0:TS], mub, ALU.mult)
            self.tt("DVE", s[:, :, 0:TS], s[:, :, 0:TS], pw[:, :, 1:1 + TS], ALU.add)
            tw = self.rot("tw", [128, 128], F32, n=1)
            self.act(tw[0:64, 0:TS], s[0:64, 4, 0:TS], AF.Tanh)
            sgs = self.rot("sgs", [128, 128], F32, n=1)
            self.act(sgs[:, 0:TS], s[:, 14, 0:TS], AF.Sigmoid)
            for b in range(4):
                self.rw_block(l, b, s, tw, sgs, H, t, TS, nch)

    def rw_block(self, l, b, s, tw, sgs, H, t, TS, nch):
        C = self.consts
        pc = self.pcol[l]
        t0 = t * TS

        def col(base):
            return pc[:, base + b:base + b + 1]

        def q(name, n=1):
            return self.rot("rw_" + name, [128, 128], F32, n=n)
        sr = s[:, b, 0:TS]
        sk = s[:, 5 + b, 0:TS]
        sv = s[:, 9 + b, 0:TS]
        ps = self.bank()
        self.mm(ps[0:128, 0:TS], self.w2[l][0:64, b * 128:(b + 1) * 128], tw[0:64, 0:TS])
        lw = q("lw")
        self.act(lw[:, 0:TS], ps[0:128, 0:TS], AF.Sigmoid, bias=col(PC_W0))
        self.ts("POOL", lw[:, 0:TS], lw[:, 0:TS], -0.6065306597126334, None, ALU.mult)
        ps = self.bank()
        self.mm(ps[0:128, 0:TS], self.a2[l][0:64, b * 128:(b + 1) * 128], s[0:64, 13, 0:TS])
        a = q("a")
        self.act(a[:, 0:TS], ps[0:128, 0:TS], AF.Sigmoid, bias=col(PC_A0))
        vf = self.vfirst[:, b, t0:t0 + TS]
        if l == 0:
            self.cp("POOL", vf, sv)
            v = vf
        else:
            ps = self.bank()
            self.mm(ps[0:128, 0:TS], self.v2[0:32, b * 128:(b + 1) * 128], s[0:32, 15, 0:TS])
            vg = q("vg")
            self.act(vg[:, 0:TS], ps[0:128, 0:TS], AF.Sigmoid, bias=col(PC_V0))
            vt = q("v")
            self.tt("DVE", vt[:, 0:TS], vf, sv, ALU.subtract)
            self.tt("DVE", vt[:, 0:TS], vt[:, 0:TS], vg[:, 0:TS], ALU.mult)
            self.tt("DVE", vt[:, 0:TS], vt[:, 0:TS], sv, ALU.add)
            v = vt[:, 0:TS]
        kk = q("kk")
        self.ts("DVE", kk[:, 0:TS], sk, col(PC_KK), None, ALU.mult)
        sq = q("sq")
        self.tt("POOL", sq[:, 0:TS], kk[:, 0:TS], kk[:, 0:TS], ALU.mult)
        ps = self.bank()
        self.mm(ps[0:128, 0:TS], C[:, CHO:CHO + 128], sq[:, 0:TS])
        self.ts("DVE", sq[:, 0:TS], ps[0:128, 0:TS], 1e-24, None, ALU.max)
        self.rsqrt(sq[:, 0:TS], sq[:, 0:TS])
        self.tt("DVE", kk[:, 0:TS], kk[:, 0:TS], sq[:, 0:TS], ALU.mult)
        km = q("km")
        self.ts("DVE", km[:, 0:TS], a[:, 0:TS], 1.0, col(PC_KA), ALU.subtract, ALU.mult)
        self.stt(km[:, 0:TS], km[:, 0:TS], 1.0, sk, ALU.add, ALU.mult)
        beta = q("beta")
        self.tt("POOL", beta[:, 0:TS], kk[:, 0:TS], a[:, 0:TS], ALU.mult)
        cl = q("cl")
        self.S.op("DVE", lambda e: e.tensor_tensor_scan(cl[:, 0:TS].ap, C[:, CRST:CRST + TS].ap, lw[:, 0:TS].ap,
                                                        0.0, ALU.mult, ALU.add),
                  reads=[C, lw], writes=[cl])

        def v3(buf):
            return buf.t[:, 0:TS].rearrange("p (c i) -> p c i", i=32)
        cl3 = v3(cl)
        ref_b = View(cl, cl3[:, :, 15:16].to_broadcast([128, nch, 32]))
        last_b = View(cl, cl3[:, :, 31:32].to_broadcast([128, nch, 32]))
        dr = q("dr")
        self.tt("DVE", View(dr, v3(dr)), View(cl, cl3), ref_b, ALU.subtract)
        er = q("er")
        self.act(er[:, 0:TS], dr[:, 0:TS], AF.Exp)
        einv = q("einv")
        self.act(einv[:, 0:TS], dr[:, 0:TS], AF.Exp, scale=-1.0)
        ea = q("ea")
        self.tt("POOL", ea[:, 0:TS], dr[:, 0:TS], lw[:, 0:TS], ALU.subtract)
        self.act(ea[:, 0:TS], ea[:, 0:TS], AF.Exp)
        el = q("el")
        self.tt("DVE", View(el, v3(el)), last_b, View(cl, cl3), ALU.subtract)
        self.act(el[:, 0:TS], el[:, 0:TS], AF.Exp)
        ea0 = q("ea0")
        self.tt("POOL", ea0[:, 0:TS], cl[:, 0:TS], lw[:, 0:TS], ALU.subtract)
        self.act(ea0[:, 0:TS], ea0[:, 0:TS], AF.Exp)
        er0 = q("er0")
        self.act(er0[:, 0:TS], cl[:, 0:TS], AF.Exp)
        gam = q("gam")
        self.act(View(gam, gam.t[:, 0:nch]), View(cl, cl3[:, :, 31]), AF.Exp)
        abar = ea
        self.stt(abar[:, 0:TS], kk[:, 0:TS], -1.0, ea[:, 0:TS], ALU.mult, ALU.mult)
        rbar = er
        self.tt("DVE", rbar[:, 0:TS], sr, er[:, 0:TS], ALU.mult)
        btil = q("btil")
        self.tt("DVE", btil[:, 0:TS], beta[:, 0:TS], einv[:, 0:TS], ALU.mult)
        ktil = einv
        self.tt("DVE", ktil[:, 0:TS], km[:, 0:TS], einv[:, 0:TS], ALU.mult)
        f4 = _Alias(self.rot("A0", [128, 512], F32, n=1), None, "p (a b) -> p a b", 4)
        self.tt("DVE", f4[:, 0, 0:TS], beta[:, 0:TS], el[:, 0:TS], ALU.mult)
        self.tt("DVE", f4[:, 1, 0:TS], km[:, 0:TS], el[:, 0:TS], ALU.mult)
        self.stt(f4[:, 2, 0:TS], kk[:, 0:TS], -1.0, ea0[:, 0:TS], ALU.mult, ALU.mult)
        self.cp("POOL", f4[:, 3, 0:TS], v)
        rb0m = self.rb0m
        for c in range(nch):
            self.tt("DVE", rb0m[:, c, c * 32:(c + 1) * 32], View(sr.b, s.t[:, b, c * 32:(c + 1) * 32]),
                    er0[:, c * 32:(c + 1) * 32], ALU.mult)
        rkr = q("rkr")
        self.stt(rkr[:, 0:TS], sr, col(PC_RK), km[:, 0:TS], ALU.mult, ALU.mult)
        psbon = self.bank(hold=True)
        self.mm(psbon[0:128, 0:TS], C[:, CHO:CHO + 128], rkr[:, 0:TS])
        bon = q("bon")
        self.tt("DVE", bon[:, 0:TS], psbon[0:128, 0:TS], v, ALU.mult)
        psbon.live = False
        ps = self.bank()
        for j in range(4):
            self.tr(ps[0:TS, j * 128:(j + 1) * 128], f4[:, j, 0:TS])
        tm4 = _Alias(self.rot("A1", [128, 512], F32, n=1), None, "p (a b) -> p a b", 4)
        self.cp("ACT", View(tm4.b, tm4.b.t[0:TS, :]), ps[0:TS, 0:512])
        bh_tm = tm4[0:TS, 0, :]
        kh_tm = tm4[0:TS, 1, :]
        ww = self.rot("rw_ww", [128, 128], F32, n=1)
        u0p = self.rot("rw_u0p", [128, 128], F32, n=1)
        arbT = []
        arkT = []
        arbs = []
        for hh in range(2):
            po = 64 * hh
            A_ = abar[po:po + 64, 0:TS]
            B_ = btil[po:po + 64, 0:TS]
            K_ = ktil[po:po + 64, 0:TS]
            R_ = rbar[po:po + 64, 0:TS]
            ps = self.bank()
            self.mm(ps[0:TS, 0:TS], B_, A_)
            self.mm(ps[0:TS, 128:128 + TS], A_, B_)
            self.mm(ps[0:TS, 256:256 + TS], K_, A_)
            nT = self.rot("rw_nT", [128, 128], F32, n=2)
            nN = self.rot("rw_nN", [128, 128], F32, n=2)
            aak = self.rot("rw_aak", [128, 128], F32, n=1)
            self.tt("DVE", nT[0:TS, 0:TS], ps[0:TS, 0:TS], C[0:TS, CMSU:CMSU + TS], ALU.mult)
            self.tt("DVE", nN[0:TS, 0:TS], ps[0:TS, 128:128 + TS], C[0:TS, CMSL:CMSL + TS], ALU.mult)
            self.tt("DVE", aak[0:TS, 0:TS], ps[0:TS, 256:256 + TS], C[0:TS, CMSU:CMSU + TS], ALU.mult)
            ps = self.bank()
            self.mm(ps[0:TS, 0:TS], B_, R_)
            self.mm(ps[0:TS, 128:128 + TS], K_, R_)
            rb = self.rot("rw_arb%d" % hh, [128, 2, 128], F32, n=1)
            self.tt("DVE", rb[0:TS, 0, 0:TS], ps[0:TS, 0:TS], C[0:TS, CM3:CM3 + TS], ALU.mult)
            self.tt("DVE", rb[0:TS, 1, 0:TS], ps[0:TS, 128:128 + TS], C[0:TS, CM3:CM3 + TS], ALU.mult)
            arbT.append(rb[0:TS, 0, 0:TS])
            arbs.append(rb)
            arkT.append(rb[0:TS, 1, 0:TS])
            ps = self.bank()
            self.mm(ps[0:TS, 0:64], aak[0:TS, 0:TS], tm4[0:TS, 3, po:po + 64])
            x = self.rot("rw_x", [128, 128], F32, n=3)
            self.cp("POOL", x[0:TS, 0:64], tm4[0:TS, 2, po:po + 64])
            self.cp("ACT", x[0:TS, 64:128], ps[0:TS, 0:64])
            for lv in range(5):
                ps = self.bank()
                self.mm(ps[0:TS, 0:128], nT[0:TS, 0:TS], x[0:TS, 0:128])
                if lv < 4:
                    ps2 = self.bank()
                    self.mm(ps2[0:TS, 0:TS], nT[0:TS, 0:TS], nN[0:TS, 0:TS])
                    self.mm(ps2[0:TS, 128:128 + TS], nN[0:TS, 0:TS], nT[0:TS, 0:TS])
                    nN = self.rot("rw_nN", [128, 128], F32, n=2)
                    nT = self.rot("rw_nT", [128, 128], F32, n=2)
                    self.cp("ACT", nN[0:TS, 0:TS], ps2[0:TS, 0:TS])
                    self.cp("ACT", nT[0:TS, 0:TS], ps2[0:TS, 128:128 + TS])
                    xn = self.rot("rw_x", [128, 128], F32, n=3)
                    self.tt("DVE", xn[0:TS, :], ps[0:TS, 0:128], x[0:TS, :], ALU.add)
                    x = xn
                else:
                    self.tt("DVE", ww[0:TS, po:po + 64], ps[0:TS, 0:64], x[0:TS, 0:64], ALU.add)
                    self.tt("DVE", u0p[0:TS, po:po + 64], ps[0:TS, 64:128], x[0:TS, 64:128], ALU.add)
        ps = self.bank()
        self.tr(ps[0:128, 0:TS], ww[0:TS, 0:128])
        wta = self.rot("rw_wta", [128, 128], F32, n=1)
        wtb = self.rot("rw_wtb", [128, 128], F32, n=1)
        self.cp("ACT", wta[0:64, 0:TS], ps[0:64, 0:TS])
        self.cp("DVE", wtb[64:128, 0:TS], ps[64:128, 0:TS])
        ya = self.bank(hold=True)
        yb = self.bank(hold=True)
        ys = (ya, yb)
        for c in range(nch):
            psu = self.bank()
            self.mm(psu[0:TS, 0:64], wta[:, 0:TS], H[:, b, 0:64])
            self.mm(psu[0:TS, 64:128], wtb[:, 0:TS], H[:, b, 64:128])
            ucm = self.rot("rw_ucm", [128, 128], F32, n=2)
            self.tt("DVE", ucm[0:TS, :], psu[0:TS, 0:128], u0p[0:TS, :], ALU.add)
            r0 = 32 * c
            for hh in range(2):
                po = 64 * hh
                self.mm(ys[hh][0:TS, 0:64], rb0m[po:po + 64, c, 0:TS], H[po:po + 64, b, po:po + 64],
                        start=(c == 0), stop=False)
                self.mm(ys[hh][0:TS, 0:64], arbs[hh][r0:r0 + 32, 0, 0:TS], ucm[r0:r0 + 32, po:po + 64], start=False, stop=False,
                        row=r0)
            psh = self.bank()
            self.mm(psh[0:128, 0:128], tm4[r0:r0 + 32, 0, :], ucm[r0:r0 + 32, :], start=True, stop=False, row=r0)
            self.mm(psh[0:128, 0:128], tm4[r0:r0 + 32, 1, :], tm4[r0:r0 + 32, 3, :], start=False, stop=True, row=r0)
            self.stt(H[:, b, :], H[:, b, :], View(gam, gam.t[:, c:c + 1]), psh[0:128, 0:128], ALU.mult, ALU.add)
        for hh in range(2):
            po = 64 * hh
            self.mm(ys[hh][0:TS, 0:64], arkT[hh], tm4[0:TS, 3, po:po + 64], start=False, stop=True)
        st = self.rot("rw_st", [128, 12], F32, n=1)
        junk = self.rot("hn_junk", [128, 128], F32, n=1)
        for hh in range(2):
            self.act(junk[0:TS, 0:64], ys[hh][0:TS, 0:64], AF.Identity, accum=st[0:TS, hh:hh + 1])
            self.act(junk[0:TS, 64:128], ys[hh][0:TS, 0:64], AF.Square, accum=st[0:TS, 2 + hh:3 + hh])
        self.ts("DVE", st[0:TS, 4:6], st[0:TS, 0:2], 1.0 / 64, None, ALU.mult)
        self.tt("DVE", st[0:TS, 6:8], st[0:TS, 4:6], st[0:TS, 4:6], ALU.mult)
        self.stt(st[0:TS, 8:10], st[0:TS, 2:4], 1.0 / 64, st[0:TS, 6:8], ALU.mult, ALU.subtract)
        self.ts("DVE", st[0:TS, 8:10], st[0:TS, 8:10], RW_GN_EPS, None, ALU.add)
        self.rsqrt(st[0:TS, 10:12], st[0:TS, 8:10])
        yn = self.rot("rw_yn", [128, 128], F32, n=1)
        for hh in range(2):
            po = 64 * hh
            self.ts("DVE", yn[0:TS, po:po + 64], ys[hh][0:TS, 0:64], st[0:TS, 4 + hh:5 + hh], st[0:TS, 10 + hh:11 + hh],
                    ALU.subtract, ALU.mult)
        ya.live = False
        yb.live = False
        ps = self.bank()
        self.tr(ps[0:128, 0:TS], yn[0:TS, 0:128])
        y1 = q("y1")
        self.act(y1[:, 0:TS], ps[0:128, 0:TS], AF.Identity, scale=col(PC_LNW), bias=col(PC_LNB))
        self.tt("DVE", y1[:, 0:TS], y1[:, 0:TS], bon[:, 0:TS], ALU.add)
        ps = self.bank()
        self.mm(ps[0:128, 0:TS], self.g2[l][:, b * 128:(b + 1) * 128], sgs[:, 0:TS])
        self.tt("DVE", self.yT_rw[:, b, t0:t0 + TS], y1[:, 0:TS], ps[0:128, 0:TS], ALU.mult)

    def merge(self, l, hT, TS, nt):
        NT = TS * nt
        for bi, (nm, yT) in enumerate((("gla", self.yT_gla), ("rw", self.yT_rw), ("hg", self.yT_hg))):
            g0 = self.wnext("gate0_" + nm)
            g1 = self.wnext("gate1_" + nm)
            wo = self.wnext("wo_" + nm)
            for t in range(nt):
                t0 = t * TS
                for hf in range(2):
                    ps = self.bank()
                    self.proj_tm(g0 if hf == 0 else g1, 0, 512, hT, t0, TS, ps[0:TS, 0:512])
                    sig = self.rot("A2", [128, 512], F32, n=1)
                    self.act(sig[0:TS, :], ps[0:TS, 0:512], AF.Sigmoid)
                    ps = self.bank()
                    for kb in range(4):
                        self.mm(ps[0:TS, 0:512], yT[:, kb, t0:t0 + TS], View(wo.b, wo.ap[:, kb, hf * 512:(hf + 1) * 512]),
                                start=(kb == 0), stop=(kb == 3))
                    mg = self.merged[t][0:TS, hf * 512:(hf + 1) * 512]
                    if bi == 0:
                        self.tt("DVE", mg, sig[0:TS, :], ps[0:TS, 0:512], ALU.mult)
                    else:
                        self.tt("DVE", sig[0:TS, :], sig[0:TS, :], ps[0:TS, 0:512], ALU.mult)
                        self.tt("POOL", mg, mg, sig[0:TS, :], ALU.add)
        mT = self.mT
        for t in range(nt):
            for half in range(2):
                ps = self.bank()
                for q in range(4):
                    kc = half * 4 + q
                    self.tr(ps[0:128, q * TS:(q + 1) * TS], self.merged[t][0:TS, kc * 128:(kc + 1) * 128])
                for q in range(4):
                    kc = half * 4 + q
                    self.cp("ACT" if q % 2 == 0 else "DVE", mT[:, kc, t * TS:(t + 1) * TS], ps[0:128, q * TS:(q + 1) * TS])
        for hf in range(2):
            wv = self.wnext("wout%d" % hf)
            for t in range(nt):
                ps = self.bank()
                for kc in range(8):
                    self.mm(ps[0:TS, 0:512], mT[:, kc, t * TS:(t + 1) * TS], View(wv.b, wv.ap[:, kc, :]),
                            start=(kc == 0), stop=(kc == 7))
                zv = self.z[t][0:TS, hf * 512:(hf + 1) * 512]
                self.tt("DVE", zv, zv, ps[0:TS, 0:512], ALU.add)

    def ffn(self, l, hT, TS, nt):
        NT = TS * nt
        pc = self.pcol[l]
        actT = self.actT
        uc = self.uc[l]
        for j in range(6):
            nb = 4 if j < 5 else 2
            wg = self.wnext("upg%d" % j)
            wv = self.wnext("upv%d" % j)
            for bb in range(nb):
                fb = j * 4 + bb
                cs = []
                for which, w_ in ((0, wg), (1, wv)):
                    ci = which * NFB + fb
                    ps = self.bank()
                    self.proj_fm(w_, bb * 128, 128, hT, NT, ps[0:128, 0:NT])
                    ub = self.rot("A%d" % which, [128, 512], F32, n=1)
                    self.cp("POOL", ub[:, 0:2], uc[:, ci, :])
                    self.cp("ACT", ub[:, 2:2 + NT], ps[0:128, 0:NT])
                    self.cp("POOL", uc[:, ci, :], ub[:, NT:NT + 2])
                    cg = self.rot("A%d" % (3 + which), [128, 512], F32, n=1)
                    self.ts("DVE", cg[:, 0:NT], ub[:, 2:2 + NT], pc[:, PC_CW2 + ci:PC_CW2 + ci + 1],
                            pc[:, PC_CB + ci:PC_CB + ci + 1], ALU.mult, ALU.add)
                    self.stt(cg[:, 0:NT], ub[:, 1:1 + NT], pc[:, PC_CW1 + ci:PC_CW1 + ci + 1], cg[:, 0:NT], ALU.mult, ALU.add)
                    self.stt(cg[:, 0:NT], ub[:, 0:NT], pc[:, PC_CW0 + ci:PC_CW0 + ci + 1], cg[:, 0:NT], ALU.mult, ALU.add)
                    cs.append(cg)
                self.act(cs[0][:, 0:NT], cs[0][:, 0:NT], AF.Silu)
                self.tt("POOL", actT[:, fb, 0:NT], cs[0][:, 0:NT], cs[1][:, 0:NT], ALU.mult)
        for hf in range(2):
            accs = [self.bank(hold=True) for _ in range(nt)]
            for j in range(6):
                nk = 4 if j < 5 else 2
                wd = self.wnext("dn%d_%d" % (hf, j))
                for t in range(nt):
                    for kk_ in range(nk):
                        fb = j * 4 + kk_
                        self.mm(accs[t][0:TS, 0:512], actT[:, fb, t * TS:(t + 1) * TS], View(wd.b, wd.ap[:, kk_, :]),
                                start=(fb == 0), stop=(fb == NFB - 1))
            for t in range(nt):
                zv = self.z[t][0:TS, hf * 512:(hf + 1) * 512]
                self.tt("DVE", zv, zv, accs[t][0:TS, 0:512], ALU.add)
                accs[t].live = False

    def group(self, sq, g):
        meta = (g == 0)
        TS = 32 if meta else 128
        nt = 1 if meta else self.ntg
        nch = 1 if meta else 4
        NT = TS * nt
        if meta:
            self.memset("POOL", self.z[0][0:32, :], 0.0)
            self.load(self.z[0][16:32, :], self.d_meta)
        else:
            r0 = (g - 1) * self.NTX
            for t in range(nt):
                self.load(self.z[t][:, :], self.d_x[sq, r0 + t * 128:r0 + (t + 1) * 128, :])
        for l in range(self.depth):
            if self.smat_layer != l:
                for bfr, src in zip((self.gkup[0], self.gkb[0], self.w2[0], self.a2[0], self.g2[0]), self.d_smats):
                    self.load(bfr[:, :], src[l])
                self.smat_layer = l
            self.norm_T(l, PC_MIXN, self.hT, TS, nt)
            hT = View(self.hT, self.hT.t[:, :, 0:NT])
            self.gla_branch(l, hT, TS, nt, nch)
            self.hg_branch(l, hT, TS, nt, nch)
            self.rw_branch(l, hT, TS, nt, nch)
            self.merge(l, hT, TS, nt)
            if meta:
                self.memset("POOL", self.z[0][0:16, :], 0.0)
            self.norm_T(l, PC_FFNN, self.hT, TS, nt)
            self.ffn(l, hT, TS, nt)
            if meta:
                self.memset("POOL", self.z[0][0:16, :], 0.0)
        if not meta:
            r0 = (g - 1) * self.NTX
            for t in range(nt):
                rstd = self.rms_rstd(self.z[t], 128)
                ob = self.merged[0]
                self.stt(ob[:, :], self.z[t][:, :], rstd, self.frow[:, :], ALU.mult, ALU.mult)
                tok = self.S.dma("SP", self.lane_of(ob), lambda e, ob=ob, t=t: e.dma_start(
                    out=self.d_y[sq, r0 + t * 128:r0 + (t + 1) * 128, :], in_=ob.t[:, :]), reads=[ob])
                self.final_toks.append(tok)

    def build(self):
        nc = bass.Bass("TRN2", target_bir_lowering=False)
        self.nc = nc
        L = self.depth

        def din(name, shape):
            return nc.dram_tensor(name, list(shape), F32, kind="ExternalInput").ap()
        self.d_x = din("x", [self.nseq, self.SEQ, D])
        self.d_meta = din("meta", [N_META, D])
        self.d_w_in = din("w_in", [L, D, W_IN])
        self.d_w_vres = din("w_in_vres", [max(L - 1, 1), D, 32])
        self.d_wo_gla = din("w_out_gla", [L, 512, D])
        self.d_wo_rw = din("w_out_rw", [L, 512, D])
        self.d_wo_hg = din("w_out_hg", [L, 512, D])
        self.d_w_out = din("w_out", [L, D, D])
        self.d_w_up = din("w_up", [L, D, 2 * D_FF])
        self.d_w_down = din("w_down", [L, D_FF, D])
        d_consts = din("consts", [128, NCC])
        d_pcol = din("pcol", [L, 128, NPC])
        d_gkup = din("gkup", [L, 16, 256])
        d_gkb = din("gkb", [L, 1, 256])
        d_w2 = din("w2", [L, 64, 512])
        d_a2 = din("a2", [L, 64, 512])
        d_g2 = din("g2", [L, 128, 512])
        d_v2 = din("v2", [1, 32, 512])
        d_lg = din("lg", [2, 128, 512])
        d_frow = din("frow", [128, D])
        self.d_y = nc.dram_tensor("y", [self.nseq, self.SEQ, D], F32, kind="ExternalOutput").ap()

        with ExitStack() as st:
            self.st = st
            S = Sched(nc, st)
            self.S = S
            S.add_engine("PE", nc.tensor)
            S.add_engine("ACT", nc.scalar)
            S.add_engine("DVE", nc.vector)
            S.add_engine("POOL", nc.gpsimd)
            S.add_queue("SP", nc.sync)
            self.rings = {}
            self.final_toks = []
            self.pbanks = [Buf(st.enter_context(nc.psum_tensor("pb%d" % i, [128, 512], F32)), "pb%d" % i) for i in range(8)]
            self.pi = -1
            self.consts = self.sb("consts", [128, NCC])
            self.load(self.consts[:, :], d_consts)
            self.pcol = [self.sb("pcol%d" % l, [128, NPC]) for l in range(L)]
            gkup = self.sb("gkup", [16, 256])
            gkb = self.sb("gkb", [1, 256])
            w2 = self.sb("w2", [64, 512])
            a2 = self.sb("a2", [64, 512])
            g2 = self.sb("g2", [128, 512])
            self.gkup, self.gkb, self.w2, self.a2, self.g2 = [gkup] * L, [gkb] * L, [w2] * L, [a2] * L, [g2] * L
            self.d_smats = (d_gkup, d_gkb, d_w2, d_a2, d_g2)
            self.smat_layer = None
            for l in range(L):
                self.load(self.pcol[l][:, :], d_pcol[l])
            self.v2 = self.sb("v2", [32, 512])
            self.load(self.v2[:, :], d_v2[0])
            self.frow = self.sb("frow", [128, D])
            self.load(self.frow[:, :], d_frow)
            self.lb = self.sb("lb", [128, 512])
            self.oml = self.sb("oml", [128, 512])
            self.load(self.lb[:, :], d_lg[1])
            self.load(self.oml[:, :], d_lg[0])
            self.tt("DVE", self.lb[:, :], self.lb[:, :], self.oml[:, :], ALU.subtract)
            self.act(self.lb[:, :], self.lb[:, :], AF.Sigmoid)
            self.ts("DVE", self.oml[:, :], self.lb[:, :], -1.0, 1.0, ALU.mult, ALU.add)
            ntg = self.ntg
            NTM = 128 * ntg
            self.z = [self.sb("z%d" % t, [128, D]) for t in range(ntg)]
            self.merged = [self.sb("mg%d" % t, [128, D]) for t in range(ntg)]
            self.hT = self.sb("hT", [128, 8, NTM], BF16)
            self.actT = self.sb("actT", [128, NFB, NTM], BF16)
            self.yT_gla = _Alias(self.actT, self.actT.t[:, 0:4, :])
            self.yT_hg = _Alias(self.actT, self.actT.t[:, 4:8, :])
            self.yT_rw = _Alias(self.actT, self.actT.t[:, 8:12, :])
            self.mT = _Alias(self.actT, self.actT.t[:, 12:20, :])
            self.vfirst = self.sb("vfirst", [128, 4, NTM])
            self.qTb = self.sb("qTb", [128, 4, 128])
            self.kTb = self.sb("kTb", [128, 4, 128])
            self.qhm = self.sb("qhm", [128, 4, 128], r=True)
            self.rb0m = self.sb("rb0m", [128, 4, 128], r=True)
            self.prw = self.sb("prw", [128, 16, 129])
            self.Sgla = [self.sb("Sgla%d" % l, [128, 512], r=True) for l in range(L)]
            self.Shg = [self.sb("Shg%d" % l, [128, 512], r=True) for l in range(L)]
            self.Hrw = [self.sb("Hrw%d" % l, [128, 4, 128], r=True) for l in range(L)]
            self.crw = [self.sb("crw%d" % l, [128, 16]) for l in range(L)]
            self.uc = [self.sb("uc%d" % l, [128, 2 * NFB, 2]) for l in range(L)]
            self.NSLOT, self.LOOK = 6, 1
            self.wslots = [self.sb("wslot%d" % i, [128, 4608], BF16) for i in range(self.NSLOT)]
            self.memset("POOL", self.qhm[:, :, :], 0.0)
            self.memset("POOL", self.rb0m[:, :, :], 0.0)
            self.memset("POOL", self.prw[:, :, :], 0.0)
            for nm in ("rw_wta", "rw_wtb"):
                bfr = self.rot(nm, [128, 128], F32, n=1)
                self.memset("POOL", bfr[:, :], 0.0)
            self.wseq = []
            for sq in range(self.nseq):
                for g in range(1 + self.nxg):
                    for l in range(L):
                        self.wseq += self.wsched_layer(l)
            self.wpos, self.wissued = 0, 0
            for sq in range(self.nseq):
                for l in range(L):
                    for bfr in (self.Sgla[l], self.Shg[l]):
                        self.memset("POOL", bfr[:, :], 0.0)
                    self.memset("POOL", self.Hrw[l][:, :, :], 0.0)
                    self.memset("POOL", self.crw[l][:, :], 0.0)
                    self.memset("POOL", self.uc[l][:, :, :], 0.0)
                for g in range(1 + self.nxg):
                    self.group(sq, g)
            assert self.wpos == len(self.wseq)
            for tok in self.final_toks:
                S.wait_tok("SP", tok)
            print("program: %d instructions, %d waits, %d sems, sbuf left %d" %
                  (S.n_ins, S.n_wait, S.nsem, nc.sbuf_bytes_remaining))
        return nc


def make_consts():
    c = np.zeros((128, NCC), np.float32)
    i = np.arange(128)
    same = (i[:, None] // 32) == (i[None, :] // 32)
    J, I = i[:, None], i[None, :]
    ref = 32 * (I // 32) + 15
    c[:, CI:CI + 128] = np.eye(128)
    c[:, CM1:CM1 + 128] = same * ((J <= I).astype(np.float32) - (J <= ref).astype(np.float32))
    c[:, CM3:CM3 + 128] = same * (J <= I)
    c[:, CMSU:CMSU + 128] = same * (J < I)
    c[:, CMSL:CMSL + 128] = same * (J > I)
    c[:, CHO:CHO + 128] = ((J // 64) == (I // 64))
    for k in range(4):
        c[:, CSEL_ + k] = (i // 32 == k)
    for k in range(2):
        c[:, CHS + k] = (i // 64 == k)
    c[:, CONE:CONE + 128] = 1.0
    c[:, CRST:CRST + 128] = (I % 32 != 0) * np.ones((128, 1))
    return c


def _cols(v, nrows=128):
    v = np.asarray(v, np.float32).reshape(-1)
    nb = (len(v) + 127) // 128
    out = np.zeros((128, nb), np.float32)
    for b in range(nb):
        seg = v[b * 128:(b + 1) * 128]
        out[:len(seg), b] = seg
    return out


def make_pcol(inp, l):
    p = np.zeros((128, NPC), np.float32)
    mu = np.asarray(inp["rw_mu"][l], np.float32)
    p[:, 0:4] = _cols(mu[0:512])
    p[:, 4:5] = _cols(mu[512:576])
    p[:, 5:9] = _cols(mu[576:1088])
    p[:, 9:13] = _cols(mu[1088:1600])
    p[:, 13:14] = _cols(mu[1600:1664])
    p[:, 14:15] = _cols(mu[1664:1792])
    if l >= 1:
        p[:, 15:16] = _cols(inp["rw_mu_vres"][l - 1])
        p[:, PC_V0:PC_V0 + 4] = _cols(inp["rw_v0"][l - 1])
    p[:, PC_W0:PC_W0 + 4] = _cols(inp["rw_w0"][l])
    p[:, PC_A0:PC_A0 + 4] = _cols(inp["rw_a0"][l])
    p[:, PC_KK:PC_KK + 4] = _cols(inp["rw_kk"][l])
    p[:, PC_KA:PC_KA + 4] = _cols(inp["rw_ka"][l])
    p[:, PC_RK:PC_RK + 4] = _cols(np.asarray(inp["rw_rk"][l]).reshape(-1))
    p[:, PC_MIXN:PC_MIXN + 8] = _cols(inp["mix_norm"][l])
    p[:, PC_FFNN:PC_FFNN + 8] = _cols(inp["ffn_norm"][l])
    p[:, PC_GLAN:PC_GLAN + 1] = _cols(inp["gla_norm"][l])
    p[:, PC_HGN:PC_HGN + 1] = _cols(inp["hg_norm"][l])
    p[:, PC_LNW:PC_LNW + 4] = _cols(inp["rw_ln_w"][l])
    p[:, PC_LNB:PC_LNB + 4] = _cols(inp["rw_ln_b"][l])
    cw = np.asarray(inp["conv_w"][l], np.float32)
    p[:, PC_CW0:PC_CW0 + 44] = _cols(cw[0])
    p[:, PC_CW1:PC_CW1 + 44] = _cols(cw[1])
    p[:, PC_CW2:PC_CW2 + 44] = _cols(cw[2])
    p[:, PC_CB:PC_CB + 44] = _cols(inp["conv_b"][l])
    return p


_PROG_CACHE = {}
_RUN_KW = {}


def run(inputs, nseq, nxg, depth, ncores, taps=()):
    key = (nseq, nxg, depth, tuple(taps))
    if key not in _PROG_CACHE:
        bld = Builder(nseq, nxg, depth, taps=taps)
        _PROG_CACHE[key] = (bld.build(), bld)
    nc, bld = _PROG_CACHE[key]
    f = lambda a: np.ascontiguousarray(np.asarray(a, np.float32))
    L = depth
    shared = {
        "meta": f(inputs["meta"]),
        "w_in": f(inputs["w_in"][:L]),
        "w_in_vres": f(inputs["w_in_vres"][:max(L - 1, 1)]),
        "w_out_gla": f(inputs["w_out_gla"][:L]),
        "w_out_rw": f(inputs["w_out_rw"][:L]),
        "w_out_hg": f(inputs["w_out_hg"][:L]),
        "w_out": f(inputs["w_out"][:L]),
        "w_up": f(inputs["w_up"][:L]),
        "w_down": f(inputs["w_down"][:L]),
        "consts": make_consts(),
        "pcol": np.stack([make_pcol(inputs, l) for l in range(L)]),
        "gkup": f(inputs["gla_gk_up"][:L]),
        "gkb": f(np.asarray(inputs["gla_gk_bias"])[:L, None, :]),
        "w2": f(inputs["rw_w2"][:L]),
        "a2": f(inputs["rw_a2"][:L]),
        "g2": f(inputs["rw_g2"][:L]),
        "v2": f(inputs["rw_v2"][:1]),
        "lg": f(np.broadcast_to(np.asarray(inputs["hg_lb_logits"], np.float32)[:2, None, :], (2, 128, 512))),
        "frow": f(np.broadcast_to(np.asarray(inputs["final_norm"], np.float32)[None, :], (128, D))),
    }
    x = f(inputs["x"])
    in_maps = []
    for c in range(ncores):
        m = dict(shared)
        m["x"] = np.ascontiguousarray(x[c * nseq:(c + 1) * nseq, :nxg * bld.NTX])
        in_maps.append(m)
    res = run_bass_kernel_spmd(nc, in_maps, core_ids=list(range(ncores)), **_RUN_KW)
    return res, bld


def kernel(**inputs):
    res, bld = run(inputs, 2, 8, DEPTH, 8)
    return np.concatenate([np.asarray(r["y"]) for r in res.results], axis=0).astype(np.float32)
```
